# Optimizing a Trainium2 kernel written in Bass

```python
import jax, jax.numpy as jnp
from jax import lax
import numpy as np

D_MODEL = 1024
BATCH = 8
SEQ = 4096
DEPTH = 4

CONV_WIDTH_A = 512
CONV_K = 3
MLA_HEADS = 8
MLA_NOPE = 64
MLA_ROPE = 32
MLA_V = 64
MLA_Q_RANK = 384
MLA_KV_RANK = 256
ROPE_THETA = 10000.0
ATTN_BLOCK = 128
GLA_HEADS = 4
GLA_DK = 128
GLA_DV = 256
GLA_GATE_RANK = 16
GLA_GATE_TAU = 16.0
GLA_CHUNK = 64
D_FF = 4 * D_MODEL
EPS = 1e-6

N_EVEN = (DEPTH + 1) // 2
N_ODD = DEPTH // 2
EVEN_SPLITS = (CONV_WIDTH_A, CONV_WIDTH_A, CONV_WIDTH_A, MLA_Q_RANK, MLA_KV_RANK, MLA_ROPE)
ODD_SPLITS = (GLA_HEADS * GLA_DK, GLA_HEADS * GLA_DK, GLA_HEADS * GLA_DV, GLA_HEADS * GLA_DV, GLA_GATE_RANK)
EVEN_IN = sum(EVEN_SPLITS)
ODD_IN = sum(ODD_SPLITS)
EVEN_MIX = CONV_WIDTH_A + MLA_HEADS * MLA_V

kernel_name = "hybrid_shortconv_mla_gla_sqrelu"


def rms_norm(x, g):
    xf = x.astype(jnp.float32)
    y = xf * lax.rsqrt(jnp.mean(xf * xf, axis=-1, keepdims=True) + EPS)
    return (y * g.astype(jnp.float32)).astype(x.dtype)


def split_cols(z, sizes):
    idx = [int(i) for i in np.cumsum(sizes)[:-1]]
    return jnp.split(z, idx, axis=-1)


def apply_rope(t, cos, sin):
    half = t.shape[-1] // 2
    t1, t2 = t[..., :half], t[..., half:]
    return jnp.concatenate([t1 * cos - t2 * sin, t2 * cos + t1 * sin], axis=-1)


def causal_short_conv(u, w):
    c = u.shape[-1]
    return lax.conv_general_dilated(
        u, w[:, None, :].astype(u.dtype), window_strides=(1,),
        padding=[(CONV_K - 1, 0)], dimension_numbers=("NWC", "WIO", "NWC"),
        feature_group_count=c)


def causal_attention_blocks(q, k, v):
    b, s, h, dq = q.shape
    dv = v.shape[-1]
    nb = s // ATTN_BLOCK
    scale = dq ** -0.5
    kt = k.transpose(0, 2, 1, 3)
    vt = v.transpose(0, 2, 1, 3)
    qb = q.reshape(b, nb, ATTN_BLOCK, h, dq).transpose(1, 0, 3, 2, 4)
    kpos = jnp.arange(s)

    def one_block(args):
        qi, i = args
        sc = jnp.einsum("bhqd,bhkd->bhqk", qi, kt, preferred_element_type=jnp.float32) * scale
        qpos = i * ATTN_BLOCK + jnp.arange(ATTN_BLOCK)
        sc = jnp.where(kpos[None, :] <= qpos[:, None], sc, -jnp.inf)
        p = jax.nn.softmax(sc, axis=-1).astype(vt.dtype)
        return jnp.einsum("bhqk,bhkd->bhqd", p, vt)

    o = lax.map(one_block, (qb, jnp.arange(nb)))
    return o.transpose(1, 0, 3, 2, 4).reshape(b, s, h * dv)


def gla_chunked(q, k, v, log_a):
    b, s, h, dk = q.shape
    dv = v.shape[-1]
    c = GLA_CHUNK
    n = s // c

    def chunks(t):
        return t.reshape(b, n, c, h, t.shape[-1]).transpose(1, 0, 3, 2, 4).astype(jnp.float32)

    qc = chunks(q) * (dk ** -0.5)
    kc = chunks(k)
    vc = chunks(v)
    bc = jnp.cumsum(chunks(log_a), axis=3)
    b_last = bc[:, :, :, -1:, :]
    q_e = qc * jnp.exp(bc)
    k_e = kc * jnp.exp(-bc)
    k_l = kc * jnp.exp(b_last - bc)
    dec = jnp.exp(b_last[:, :, :, 0, :])
    causal = jnp.tril(jnp.ones((c, c), dtype=bool))
    attn = jnp.where(causal, jnp.einsum("nbhid,nbhjd->nbhij", q_e, k_e), 0.0)
    o_intra = jnp.einsum("nbhij,nbhjv->nbhiv", attn, vc)

    def step(state, inp):
        q_n, k_n, v_n, dec_n = inp
        o_n = jnp.einsum("bhid,bhdv->bhiv", q_n, state)
        state = dec_n[..., None] * state + jnp.einsum("bhjd,bhjv->bhdv", k_n, v_n)
        return state, o_n

    state0 = jnp.zeros((b, h, dk, dv), jnp.float32)
    _, o_inter = lax.scan(step, state0, (q_e, k_l, vc, dec))
    o = (o_intra + o_inter).transpose(1, 0, 3, 2, 4).reshape(b, s, h, dv)
    return o.astype(v.dtype)


def even_mixer(h, cos, sin, w_in, conv_w, q_norm_g, w_qb, kv_norm_g, w_kvb, w_out):
    b, s, _ = h.shape
    z = h @ w_in
    a_b, a_c, a_v, z_q, z_kv, z_pe = split_cols(z, EVEN_SPLITS)
    y_a = a_b * causal_short_conv(a_c * a_v, conv_w)
    q = (rms_norm(z_q, q_norm_g) @ w_qb).reshape(b, s, MLA_HEADS, MLA_NOPE + MLA_ROPE)
    q = jnp.concatenate([q[..., :MLA_NOPE],
                         apply_rope(q[..., MLA_NOPE:], cos[:, :, None, :], sin[:, :, None, :])], axis=-1)
    kv = (rms_norm(z_kv, kv_norm_g) @ w_kvb).reshape(b, s, MLA_HEADS, MLA_NOPE + MLA_V)
    k_pe = apply_rope(z_pe, cos, sin)
    k = jnp.concatenate([kv[..., :MLA_NOPE],
                         jnp.broadcast_to(k_pe[:, :, None, :], (b, s, MLA_HEADS, MLA_ROPE))], axis=-1)
    y_b = causal_attention_blocks(q, k, kv[..., MLA_NOPE:])
    return jnp.concatenate([y_a, y_b], axis=-1) @ w_out


def odd_mixer(h, w_in, w_gate2, b_gate2, o_norm_g, w_out):
    b, s, _ = h.shape
    z = h @ w_in
    q, k, v, g, g_low = split_cols(z, ODD_SPLITS)
    log_a = jax.nn.log_sigmoid((g_low @ w_gate2 + b_gate2).astype(jnp.float32)) / GLA_GATE_TAU
    o = gla_chunked(q.reshape(b, s, GLA_HEADS, GLA_DK), k.reshape(b, s, GLA_HEADS, GLA_DK),
                    v.reshape(b, s, GLA_HEADS, GLA_DV), log_a.reshape(b, s, GLA_HEADS, GLA_DK))
    o = rms_norm(o, o_norm_g).reshape(b, s, GLA_HEADS * GLA_DV)
    return (o * jax.nn.silu(g)) @ w_out


def setup_inputs(seed: int = 0) -> dict:
    key = jax.random.key(seed)
    ks = jax.random.split(key, 20)
    resid = (2.0 * DEPTH) ** -0.5

    def nrm(k, shape, fan_in, scale=1.0):
        return jax.random.normal(k, shape, jnp.float32) * (scale * fan_in ** -0.5)

    def gain(k, shape):
        return 1.0 + 0.02 * jax.random.normal(k, shape, jnp.float32)

    x = jax.random.normal(ks[0], (BATCH, SEQ, D_MODEL), jnp.float32)
    offsets = jax.random.randint(ks[1], (BATCH, 1), 0, 4096, dtype=jnp.int32)
    positions = (offsets + jnp.arange(SEQ, dtype=jnp.int32)[None, :]).astype(jnp.int32)
    return {
        "x": x,
        "positions": positions,
        "mix_norm_g": gain(ks[2], (DEPTH, D_MODEL)),
        "mlp_norm_g": gain(ks[3], (DEPTH, D_MODEL)),
        "final_norm_g": gain(ks[4], (D_MODEL,)),
        "ev_w_in": nrm(ks[5], (N_EVEN, D_MODEL, EVEN_IN), D_MODEL),
        "ev_conv_w": nrm(ks[6], (N_EVEN, CONV_K, CONV_WIDTH_A), CONV_K),
        "ev_q_norm_g": gain(ks[7], (N_EVEN, MLA_Q_RANK)),
        "ev_w_qb": nrm(ks[8], (N_EVEN, MLA_Q_RANK, MLA_HEADS * (MLA_NOPE + MLA_ROPE)), MLA_Q_RANK),
        "ev_kv_norm_g": gain(ks[9], (N_EVEN, MLA_KV_RANK)),
        "ev_w_kvb": nrm(ks[10], (N_EVEN, MLA_KV_RANK, MLA_HEADS * (MLA_NOPE + MLA_V)), MLA_KV_RANK),
        "ev_w_out": nrm(ks[11], (N_EVEN, EVEN_MIX, D_MODEL), EVEN_MIX, resid),
        "od_w_in": nrm(ks[12], (N_ODD, D_MODEL, ODD_IN), D_MODEL),
        "od_w_gate2": nrm(ks[13], (N_ODD, GLA_GATE_RANK, GLA_HEADS * GLA_DK), GLA_GATE_RANK),
        "od_b_gate2": 0.1 * jax.random.normal(ks[14], (N_ODD, GLA_HEADS * GLA_DK), jnp.float32),
        "od_o_norm_g": gain(ks[15], (N_ODD, GLA_DV)),
        "od_w_out": nrm(ks[16], (N_ODD, GLA_HEADS * GLA_DV, D_MODEL), GLA_HEADS * GLA_DV, resid),
        "mlp_w1": nrm(ks[17], (DEPTH, D_MODEL, D_FF), D_MODEL),
        "mlp_w2": nrm(ks[18], (DEPTH, D_FF, D_MODEL), D_FF, resid),
    }


def reference(x, positions, mix_norm_g, mlp_norm_g, final_norm_g, ev_w_in, ev_conv_w,
              ev_q_norm_g, ev_w_qb, ev_kv_norm_g, ev_w_kvb, ev_w_out, od_w_in, od_w_gate2,
              od_b_gate2, od_o_norm_g, od_w_out, mlp_w1, mlp_w2):
    inv_freq = 1.0 / (ROPE_THETA ** (jnp.arange(0, MLA_ROPE, 2, dtype=jnp.float32) / MLA_ROPE))
    ang = positions.astype(jnp.float32)[..., None] * inv_freq
    cos = jnp.cos(ang).astype(x.dtype)
    sin = jnp.sin(ang).astype(x.dtype)
    for layer in range(DEPTH):
        j = layer // 2
        h = rms_norm(x, mix_norm_g[layer])
        if layer % 2 == 0:
            x = x + even_mixer(h, cos, sin, ev_w_in[j], ev_conv_w[j], ev_q_norm_g[j], ev_w_qb[j],
                               ev_kv_norm_g[j], ev_w_kvb[j], ev_w_out[j])
        else:
            x = x + odd_mixer(h, od_w_in[j], od_w_gate2[j], od_b_gate2[j], od_o_norm_g[j], od_w_out[j])
        h = rms_norm(x, mlp_norm_g[layer])
        x = x + jnp.square(jax.nn.relu(h @ mlp_w1[layer])) @ mlp_w2[layer]
    return rms_norm(x, final_norm_g)
```

```python
import math
from contextlib import ExitStack, contextmanager
import numpy as np
import concourse.bass as bass
import concourse.mybir as mybir
from concourse.bass_utils import run_bass_kernel_spmd

F32, BF16, I32 = mybir.dt.float32, mybir.dt.bfloat16, mybir.dt.int32
AF = mybir.ActivationFunctionType
ALU = mybir.AluOpType

S = 4096
D = 1024
T = 512
NT = S // T
EPS = 1e-6
TWO_PI = 2.0 * math.pi


class Buf:
    __slots__ = ("a", "w", "r", "ds")

    def __init__(self, a):
        self.a = a
        self.w = None
        self.r = {}
        self.ds = None


class KB:
    def __init__(self, nc):
        self.nc = nc
        self.es = ExitStack()
        self.eng = {"pe": nc.tensor, "act": nc.scalar, "dve": nc.vector, "pool": nc.gpsimd, "sp": nc.sync}
        self.sem = {}
        self.cnt = {}
        self.seen = {e: {} for e in self.eng}
        self.pend = {e: ([], []) for e in self.eng}
        for e in ("pe", "act", "dve", "pool"):
            self.sem[e] = self.es.enter_context(nc.semaphore("c_" + e))
            self.cnt[e] = 0
        self.dsems = []
        self.free_ds = {}
        self.stage_bufs = None
        self.uid = 0
        self.psi = 0
        self.psb = []

    def _name(self, p):
        self.uid += 1
        return f"{p}{self.uid}"

    def sb(self, shape, dtype):
        t = self.es.enter_context(self.nc.sbuf_tensor(self._name("sb"), list(shape), dtype))
        b = Buf(t[:])
        if self.stage_bufs is not None:
            self.stage_bufs.append(b)
        return b

    def sbc(self, n, inner, dtype, parts=128):
        t = self.es.enter_context(self.nc.sbuf_tensor(self._name("sbc"), [parts, n] + list(inner), dtype))
        full = t[:]
        ch = [Buf(full[:, c]) for c in range(n)]
        if self.stage_bufs is not None:
            self.stage_bufs.extend(ch)
        return full, ch

    def ps_alloc(self, n):
        self.psb = []
        for _ in range(n):
            t = self.es.enter_context(self.nc.psum_tensor(self._name("ps"), [128, 512], F32))
            self.psb.append(Buf(t[:]))
        self.psi = 0

    def ps(self):
        b = self.psb[self.psi % len(self.psb)]
        self.psi += 1
        return b

    def _ds(self, b, q):
        kind = "sw" if q == "pool" else "hw"
        if b.ds is None:
            b.ds = {}
        if kind not in b.ds:
            fl = self.free_ds.setdefault(kind, [])
            if fl:
                b.ds[kind] = fl.pop()
            else:
                sm = self.es_root.enter_context(self.nc.semaphore(self._name("d" + kind)))
                d = [sm, 0]
                b.ds[kind] = d
                self.dsems.append(d)
        return b.ds[kind]

    def _wait(self, e, toks):
        en = self.eng[e]
        seen = self.seen[e]
        for tk in toks:
            if tk is None:
                continue
            sm, v = tk
            if seen.get(sm, 0) >= v:
                continue
            seen[sm] = v
            en.wait_ge(sm, v)

    @staticmethod
    def _deps(reads, writes):
        d = []
        for b in reads:
            d.append(b.w)
        for b in writes:
            d.append(b.w)
            d.extend(b.r.values())
        return d

    def op(self, e, fn, reads=(), writes=(), sig=True):
        self._wait(e, self._deps(reads, writes))
        ins = fn(self.eng[e])
        pr, pw = self.pend[e]
        pr.extend(reads)
        pw.extend(writes)
        if sig:
            self.cnt[e] += 1
            sm = self.sem[e]
            tk = (sm, self.cnt[e])
            ins.then_inc(sm, 1)
            for b in pr:
                b.r[sm] = tk
            for b in pw:
                b.w = tk
                b.r = {}
            pr.clear()
            pw.clear()
            return tk
        return None

    def dma(self, q, pairs, reads=(), writes=(), sbuf=None):
        self._wait(q, self._deps(reads, writes))
        ds = self._ds(sbuf, q)
        for (o, i) in pairs:
            self.eng[q].dma_start(out=o, in_=i).then_inc(ds[0], 16)
            ds[1] += 16
        tk = (ds[0], ds[1])
        for b in reads:
            b.r[ds[0]] = tk
        for b in writes:
            b.w = tk
            b.r = {}
        return tk

    def barrier(self):
        for e in self.eng:
            assert not self.pend[e][0] and not self.pend[e][1], e
        toks = [(self.sem[e], self.cnt[e]) for e in self.sem if self.cnt[e] > 0]
        toks += [(d[0], d[1]) for d in self.dsems if d[1] > 0]
        for e in self.eng:
            self._wait(e, toks)

    @contextmanager
    def stage(self):
        outer = self.es
        self.es = ExitStack()
        self.stage_bufs = []
        try:
            yield
        finally:
            self.barrier()
            for b in self.stage_bufs:
                if b.ds is not None:
                    for kind, d in b.ds.items():
                        self.free_ds.setdefault(kind, []).append(d)
                    b.ds = None
            self.stage_bufs = None
            self.es.close()
            self.es = outer

    def mm(self, out, lhsT, rhs, start, stop, reads, writes, sig=None):
        if sig is None:
            sig = stop
        return self.op("pe", lambda e: e.matmul(out, lhsT=lhsT, rhs=rhs, start=start, stop=stop),
                       reads=reads, writes=writes, sig=sig)

    def act(self, out, in_, func, reads, writes, **kw):
        return self.op("act", lambda e: e.activation(out=out, in_=in_, func=func, **kw), reads=reads, writes=writes)

    def tt(self, e, out, in0, in1, op, reads, writes):
        return self.op(e, lambda en: en.tensor_tensor(out=out, in0=in0, in1=in1, op=op), reads=reads, writes=writes)

    def stt(self, e, out, in0, scalar, in1, op0, op1, reads, writes):
        return self.op(e, lambda en: en.scalar_tensor_tensor(out=out, in0=in0, scalar=scalar, in1=in1, op0=op0, op1=op1),
                       reads=reads, writes=writes)

    def ts(self, e, out, in0, s1, s2, op0, op1, reads, writes):
        if s2 is None:
            return self.op(e, lambda en: en.tensor_scalar(out=out, in0=in0, scalar1=s1, scalar2=None, op0=op0),
                           reads=reads, writes=writes)
        return self.op(e, lambda en: en.tensor_scalar(out=out, in0=in0, scalar1=s1, scalar2=s2, op0=op0, op1=op1),
                       reads=reads, writes=writes)

    def cp(self, e, out, in_, reads, writes):
        if e == "act":
            return self.op(e, lambda en: en.copy(out=out, in_=in_), reads=reads, writes=writes)
        return self.op(e, lambda en: en.tensor_copy(out=out, in_=in_), reads=reads, writes=writes)


def cview(ap):
    return ap.rearrange("(c p) s -> p c s", p=128)


def norm_a(k, C, src_aps, src_bufs, sqc):
    for c in range(C):
        k.act(sqc[c].a, src_aps[c], AF.Square, reads=[src_bufs[c]], writes=[sqc[c]])


def norm_b(k, C, src_aps, src_bufs, gain, Dn, sqc, outc, rs, ones, gbuf, in_scale=None, bias=None, eng="dve"):
    if in_scale is None:
        in_scale = 1.0 / Dn
    if bias is None:
        bias = EPS
    norm_stats(k, C, sqc, rs, ones, in_scale, bias)
    for c in range(C):
        k.stt(eng, outc[c].a, src_aps[c], gain(c), rs.a, ALU.mult, ALU.mult,
              reads=[src_bufs[c], rs, gbuf], writes=[outc[c]])


def norm_stats(k, C, sqc, rs, ones, in_scale, bias):
    pb = k.ps()
    for c in range(C):
        k.mm(pb.a, ones.a, sqc[c].a, c == 0, c == C - 1, reads=[ones, sqc[c]], writes=[pb])
    k.act(rs.a, pb.a, AF.Ln, reads=[pb], writes=[rs], scale=in_scale, bias=bias)
    k.act(rs.a, rs.a, AF.Exp, reads=[rs], writes=[rs], scale=-0.5)


def rmsnorm_fm(k, C, src_aps, src_bufs, gain, Dn, sqc, outc, rs, ones, gbuf, in_scale=None, bias=None):
    norm_a(k, C, src_aps, src_bufs, sqc)
    norm_b(k, C, src_aps, src_bufs, gain, Dn, sqc, outc, rs, ones, gbuf, in_scale, bias)


def mlp_stage(k, G, L, x_src, x_dst, mix_w, YT, final, outT):
    w1, w2 = G["w1"], G["w2"]
    with k.stage():
        k.ps_alloc(8)
        xts = [k.sbc(8, [T], F32) for _ in range(3)]
        yts = [k.sbc(8, [T], BF16) for _ in range(2)]
        _, woc = k.sbc(8, [1024], BF16)
        for oc in range(8):
            k.dma("pool", [(woc[oc].a, mix_w[oc])], writes=[woc[oc]], sbuf=woc[oc])
        _, sqc = k.sbc(8, [T], BF16)
        hcs = [k.sbc(8, [T], BF16)[1] for _ in range(2)]
        _, hidc = k.sbc(32, [T], BF16)
        rs = k.sb([128, T], F32)
        rl = [k.sb([128, T], BF16) for _ in range(3)]
        w1b = [k.sb([128, 4, 1024], BF16) for _ in range(3)]
        w2b = [k.sb([128, 4096], BF16) for _ in range(3)]
        gm = G["gmlp"]

        def pro_load(t):
            sl = slice(t * T, (t + 1) * T)
            xt, xc = xts[t % 3]
            yt, yc = yts[t % 2]
            k.dma("sp", [(xt, cview(x_src)[:, :, sl])], writes=xc, sbuf=xc[0])
            k.dma("sp", [(yt, cview(YT)[:, :, sl])], writes=yc, sbuf=yc[0])

        def pro_a(t):
            xt, xc = xts[t % 3]
            yt, yc = yts[t % 2]
            for oc in range(8):
                pb = k.ps()
                for kc in range(8):
                    k.mm(pb.a, woc[oc].a[:, kc * 128:(kc + 1) * 128], yc[kc].a, kc == 0, kc == 7,
                         reads=[woc[oc], yc[kc]], writes=[pb])
                k.tt("dve", xc[oc].a, pb.a, xc[oc].a, ALU.add, reads=[pb, xc[oc]], writes=[xc[oc]])
            norm_a(k, 8, [b.a for b in xc], xc, sqc)

        def pro_b(t):
            xt, xc = xts[t % 3]
            norm_b(k, 8, [b.a for b in xc], xc, lambda c: gm.a[:, L * 8 + c:L * 8 + c + 1], D,
                   sqc, hcs[t % 2], rs, G["ones"], gm)

        pro_load(0)
        pro_a(0)
        pro_b(0)
        wi = 0
        for t in range(NT):
            sl = slice(t * T, (t + 1) * T)
            xt, xc = xts[t % 3]
            hc = hcs[t % 2]
            if t + 1 < NT:
                pro_load(t + 1)
            for og in range(8):
                wb = w1b[wi % 3]
                wi += 1
                k.dma("pool", [(wb.a, w1[L, og * 4:(og + 1) * 4].rearrange("o p n -> p o n"))], writes=[wb], sbuf=wb)
                for o4 in range(4):
                    oc = og * 4 + o4
                    pb = k.ps()
                    for kc in range(8):
                        k.mm(pb.a, wb.a[:, o4, kc * 128:(kc + 1) * 128], hc[kc].a, kc == 0, kc == 7,
                             reads=[wb, hc[kc]], writes=[pb])
                    r = rl[oc % 3]
                    k.act(r.a, pb.a, AF.Relu, reads=[pb], writes=[r])
                    k.tt("dve", hidc[oc].a, r.a, r.a, ALU.mult, reads=[r], writes=[hidc[oc]])
                if og == 5 and t + 1 < NT:
                    pro_a(t + 1)
            for oc in range(8):
                wb = w2b[(t * 8 + oc) % 3]
                k.dma("pool", [(wb.a, w2[L, oc])], writes=[wb], sbuf=wb)
                pb = k.ps()
                for kc in range(32):
                    k.mm(pb.a, wb.a[:, kc * 128:(kc + 1) * 128], hidc[kc].a, kc == 0, kc == 31,
                         reads=[wb, hidc[kc]], writes=[pb])
                k.tt("dve", xc[oc].a, pb.a, xc[oc].a, ALU.add, reads=[pb, xc[oc]], writes=[xc[oc]])
                if oc == 2 and t + 1 < NT:
                    pro_b(t + 1)
            if final:
                rmsnorm_fm(k, 8, [b.a for b in xc], xc, lambda c: G["gfin"].a[:, c:c + 1], D,
                           sqc, xc, rs, G["ones"], G["gfin"])
                k.dma("sp", [(cview(outT)[:, :, sl], xt)], reads=xc, sbuf=xc[0])
            else:
                k.dma("sp", [(cview(x_dst)[:, :, sl], xt)], reads=xc, sbuf=xc[0])


def even_stage1(k, G, L, j, x_src, YT, QT, KT, Vfull, Vc):
    pos = G["pos"]
    with k.stage():
        k.ps_alloc(8)
        xts = [k.sbc(8, [T], F32) for _ in range(2)]
        _, sqc = k.sbc(8, [T], BF16)
        _, sqz = k.sbc(3, [T], BF16)
        hcs = [k.sbc(8, [T], BF16)[1] for _ in range(2)]
        rs = k.sb([128, T], F32)
        rsz = k.sb([128, T], F32)
        _, winc = k.sbc(19, [1024], BF16)
        wq = k.sb([128, 8, 288], BF16)
        wqs = k.sb([128, 8, 288], BF16)
        wkk = k.sb([128, 8, 128], BF16)
        wv = k.sb([128, 1024], BF16)
        worder = [12, 13, 14, 0, 4, 8, 1, 5, 9, 15, 16, 2, 6, 10, 3, 7, 11, 17, 18]
        for n_, oc in enumerate(worder):
            k.dma("pool", [(winc[oc].a, G["ev_win"][j, oc])], writes=[winc[oc]], sbuf=winc[oc])
            if n_ == 8:
                for (dst, src) in ((wq, G["ev_wq"][j]), (wqs, G["ev_wqs"][j])):
                    k.dma("pool", [(dst.a, src.rearrange("o p n -> p o n"))], writes=[dst], sbuf=dst)
        k.dma("pool", [(wkk.a, G["ev_wkk"][j].rearrange("o p n -> p o n"))], writes=[wkk], sbuf=wkk)
        k.dma("pool", [(wv.a, G["ev_wv"][j])], writes=[wv], sbuf=wv)
        ufull, uc = k.sbc(4, [T + 2], F32)
        for c in range(4):
            k.op("pool", lambda e: e.memset(uc[c].a[:, 0:2], 0.0), writes=[uc[c]])
        acs = [k.sb([128, T], F32) for _ in range(1)]
        acc = [k.sb([128, T], F32) for _ in range(1)]
        yats = [k.sbc(4, [T], BF16) for _ in range(1)]
        zqn = k.sbc(3, [T], BF16)[1]
        zkvn = k.sbc(2, [T], BF16)[1]
        qt = k.sb([96, 8, T], BF16)
        kts = k.sb([96, 8, T], BF16)
        posi = k.sb([128, T], I32)
        ang = k.sb([128, T], F32)
        rtmp = k.sb([128, T], F32)
        cosT = k.sb([128, T], F32)
        sinS = k.sb([128, T], F32)
        t1s = [k.sb([128, T], F32) for _ in range(2)]
        t2s = [k.sb([128, T], F32) for _ in range(2)]
        kpe = k.sb([128, T], BF16)
        crope = G["crope"]
        gmx = G["gmix"]
        cwb = G["cw"]
        R = slice(64, 96)

        def pro_load(t):
            sl = slice(t * T, (t + 1) * T)
            xt, xc = xts[t % 2]
            k.dma("sp", [(xt, cview(x_src)[:, :, sl])], writes=xc, sbuf=xc[0])

        def pro_a(t):
            xt, xc = xts[t % 2]
            norm_a(k, 8, [b.a for b in xc], xc, sqc)

        def pro_b(t):
            xt, xc = xts[t % 2]
            norm_b(k, 8, [b.a for b in xc], xc, lambda c: gmx.a[:, L * 8 + c:L * 8 + c + 1], D,
                   sqc, hcs[t % 2], rs, G["ones"], gmx)

        def rope_steps(t):
            sl = slice(t * T, (t + 1) * T)
            st = []
            st.append(lambda: k.dma("sp", [(posi.a[R, :], pos[0:1, sl].partition_broadcast(32))], writes=[posi], sbuf=posi))
            st.append(lambda: k.cp("dve", ang.a[R, :], posi.a[R, :], reads=[posi], writes=[ang]))
            st.append(lambda: k.ts("dve", ang.a[R, :], ang.a[R, :], crope.a[R, 0:1], None, ALU.mult, None, reads=[ang, crope], writes=[ang]))
            for (shift, dst) in ((0.0, sinS), (math.pi / 2, cosT)):
                st.append(lambda shift=shift: k.ts("dve", rtmp.a[R, :], ang.a[R, :], shift, 1.0 / TWO_PI, ALU.add, ALU.mult, reads=[ang], writes=[rtmp]))
                st.append(lambda: k.cp("dve", posi.a[R, :], rtmp.a[R, :], reads=[rtmp], writes=[posi]))
                st.append(lambda: k.cp("dve", rtmp.a[R, :], posi.a[R, :], reads=[posi], writes=[rtmp]))
                st.append(lambda: k.stt("dve", rtmp.a[R, :], rtmp.a[R, :], -TWO_PI, ang.a[R, :], ALU.mult, ALU.add, reads=[rtmp, ang], writes=[rtmp]))
                st.append(lambda shift=shift: k.ts("dve", rtmp.a[R, :], rtmp.a[R, :], -math.pi - shift, math.pi - shift, ALU.max, ALU.min, reads=[rtmp], writes=[rtmp]))
                if shift == 0.0:
                    st.append(lambda: k.act(rtmp.a[R, :], rtmp.a[R, :], AF.Sin, reads=[rtmp], writes=[rtmp]))
                    st.append(lambda dst=dst: k.ts("dve", dst.a[R, :], rtmp.a[R, :], crope.a[R, 1:2], None, ALU.mult, None, reads=[rtmp, crope], writes=[dst]))
                else:
                    st.append(lambda dst=dst: k.act(dst.a[R, :], rtmp.a[R, :], AF.Sin, reads=[rtmp, G["hpi"]], writes=[dst], bias=G["hpi"].a[R, 0:1]))
            return st

        def prob_steps(t):
            xt, xc = xts[t % 2]
            st = [lambda: norm_stats(k, 8, sqc, rs, G["ones"], 1.0 / D, EPS)]
            for c in range(8):
                st.append(lambda c=c: k.stt("dve", hcs[t % 2][c].a, xc[c].a, gmx.a[:, L * 8 + c:L * 8 + c + 1], rs.a, ALU.mult, ALU.mult,
                                            reads=[xc[c], rs, gmx], writes=[hcs[t % 2][c]]))
            return st

        def rope_tables(t):
            for f in rope_steps(t):
                f()

        pending = []

        def emit(n):
            for _ in range(n):
                if pending:
                    pending.pop(0)()

        pro_load(0)
        pro_a(0)
        pro_b(0)
        rope_tables(0)
        for t in range(NT):
            sl = slice(t * T, (t + 1) * T)
            hc = hcs[t % 2]
            if t + 1 < NT:
                pro_load(t + 1)

            def proj(oc_idx):
                pb = k.ps()
                for kc in range(8):
                    k.mm(pb.a, winc[oc_idx].a[:, kc * 128:(kc + 1) * 128], hc[kc].a, kc == 0, kc == 7,
                         reads=[winc[oc_idx], hc[kc]], writes=[pb])
                return pb

            yat, yac = yats[0]

            def conv(c):
                pb_b = proj(c)
                pb_c = proj(4 + c)
                pb_v = proj(8 + c)
                a_s = acs[0]
                ac = acc[0]
                base = (j * 4 + c) * 3
                k.cp("act", a_s.a, pb_c.a, reads=[pb_c], writes=[a_s])
                k.tt("dve", uc[c].a[:, 2:T + 2], pb_v.a, a_s.a, ALU.mult, reads=[pb_v, a_s], writes=[uc[c]])
                k.ts("dve", ac.a, uc[c].a[:, 0:T], cwb.a[:, base:base + 1], None, ALU.mult, None,
                     reads=[uc[c], cwb], writes=[ac])
                k.stt("dve", ac.a, uc[c].a[:, 1:T + 1], cwb.a[:, base + 1:base + 2], ac.a, ALU.mult, ALU.add,
                      reads=[uc[c], cwb, ac], writes=[ac])
                k.stt("dve", ac.a, uc[c].a[:, 2:T + 2], cwb.a[:, base + 2:base + 3], ac.a, ALU.mult, ALU.add,
                      reads=[uc[c], cwb, ac], writes=[ac])
                k.tt("dve", yac[c].a, pb_b.a, ac.a, ALU.mult, reads=[pb_b, ac], writes=[yac[c]])
                k.cp("act", uc[c].a[:, 0:2], uc[c].a[:, T:T + 2], reads=[uc[c]], writes=[uc[c]])

            pzq = [proj(12 + c) for c in range(3)]
            norm_a(k, 3, [b.a for b in pzq], pzq, sqz)
            conv(0)
            norm_b(k, 3, [b.a for b in pzq], pzq, lambda c: G["gq"].a[:, j * 3 + c:j * 3 + c + 1], 384,
                   sqz, zqn, rsz, G["ones"], G["gq"], in_scale=96.0 / 384.0, bias=EPS * 96.0)
            conv(1)
            pzkv = [proj(15 + c) for c in range(2)]
            norm_a(k, 2, [b.a for b in pzkv], pzkv, sqz)
            conv(2)
            norm_b(k, 2, [b.a for b in pzkv], pzkv, lambda c: G["gkv"].a[:, j * 2 + c:j * 2 + c + 1], 256,
                   sqz, zkvn, rsz, G["ones"], G["gkv"])
            conv(3)
            k.dma("sp", [(cview(YT)[:, 0:4, sl], yat)], reads=yac, sbuf=yac[0])
            if t + 1 < NT:
                pro_a(t + 1)
            ppe = proj(17)
            ppes = proj(18)
            t1 = t1s[0]
            t2 = t2s[0]
            k.tt("dve", t1.a[R, :], ppe.a[R, :], cosT.a[R, :], ALU.mult, reads=[ppe, cosT], writes=[t1])
            k.tt("dve", t2.a[R, :], ppes.a[R, :], sinS.a[R, :], ALU.mult, reads=[ppes, sinS], writes=[t2])
            k.tt("dve", kpe.a[R, :], t1.a[R, :], t2.a[R, :], ALU.add, reads=[t1, t2], writes=[kpe])
            for h in range(8):
                pq = k.ps()
                for kc in range(3):
                    k.mm(pq.a[0:96, :], wq.a[:, h, kc * 96:(kc + 1) * 96], zqn[kc].a, kc == 0, kc == 2,
                         reads=[wq, zqn[kc]], writes=[pq])
                pqs = k.ps()
                for kc in range(3):
                    k.mm(pqs.a[0:96, :], wqs.a[:, h, kc * 96:(kc + 1) * 96], zqn[kc].a, kc == 0, kc == 2,
                         reads=[wqs, zqn[kc]], writes=[pqs])
                k.cp("act", qt.a[0:64, h, :], pq.a[0:64, :], reads=[pq], writes=[qt])
                t1 = t1s[h % 2]
                t2 = t2s[h % 2]
                k.tt("dve", t1.a[R, :], pq.a[R, :], cosT.a[R, :], ALU.mult, reads=[pq, cosT], writes=[t1])
                k.tt("dve", t2.a[R, :], pqs.a[R, :], sinS.a[R, :], ALU.mult, reads=[pqs, sinS], writes=[t2])
                k.tt("pool", qt.a[R, h, :], t1.a[R, :], t2.a[R, :], ALU.add, reads=[t1, t2], writes=[qt])
                if h == 0 and t + 1 < NT:
                    pending.extend(prob_steps(t + 1))
                emit(2)
            k.dma("sp", [(QT.rearrange("h p s -> p h s")[:, :, sl], qt.a)], reads=[qt], sbuf=qt)
            if t + 1 < NT:
                pending.extend(rope_steps(t + 1))
            for h in range(8):
                pk = k.ps()
                for kc in range(2):
                    k.mm(pk.a[0:64, :], wkk.a[:, h, kc * 64:(kc + 1) * 64], zkvn[kc].a, kc == 0, kc == 1,
                         reads=[wkk, zkvn[kc]], writes=[pk])
                k.cp("act", kts.a[0:64, h, :], pk.a[0:64, :], reads=[pk], writes=[kts])
                k.cp("pool", kts.a[R, h, :], kpe.a[R, :], reads=[kpe], writes=[kts])
                emit(2)
            k.dma("sp", [(KT.rearrange("h p s -> p h s")[:, :, sl], kts.a)], reads=[kts], sbuf=kts)
            for b in range(4):
                pv = k.ps()
                for kc in range(2):
                    k.mm(pv.a, zkvn[kc].a[:, b * 128:(b + 1) * 128], wv.a[:, kc * 512:(kc + 1) * 512], kc == 0, kc == 1,
                         reads=[wv, zkvn[kc]], writes=[pv])
                k.cp("act", Vfull[:, t * 4 + b, :, :], pv.a.rearrange("p (h d) -> p h d", d=64),
                     reads=[pv], writes=[Vc[t]])
                emit(3)
            emit(len(pending))


def even_stage2(k, G, YT, QT, KT, Vfull, Vc):
    LA = 2
    with k.stage():
        accs = []
        for _ in range(2):
            t_ = k.es.enter_context(k.nc.psum_tensor(k._name("pa"), [128, 512], F32))
            accs.append(Buf(t_[:]))
        NB = 3
        pss = []
        for _ in range(NB):
            t_ = k.es.enter_context(k.nc.psum_tensor(k._name("pp"), [128, 1024], F32))
            pss.append(Buf(t_[:]))
        qhs = [k.sb([96, S], BF16) for _ in range(2)]
        khs = [k.sb([96, S], BF16) for _ in range(2)]
        vas = [k.sb([128, 32, 128], BF16) for _ in range(2)]
        for va in vas:
            k.op("pool", lambda e: e.memset(va.a[:, :, 64:128], 1.0), writes=[va])
        pts = [k.sb([128, 2 * T], BF16) for _ in range(NB)]
        rden = [k.sb([64, T], F32) for _ in range(2)]
        ybs = [k.sb([64, T], BF16) for _ in range(2)]
        atri = G["atri"]
        iters = [(h, g, kp) for h in range(8) for g in range(NT) for kp in range(2 * g + 2)]
        N = len(iters)

        def front(i):
            h, g, kp = iters[i]
            qh = qhs[h % 2]
            kh = khs[h % 2]
            va = vas[h % 2]
            if g == 0 and kp == 0:
                k.dma("sp", [(qh.a, QT[h])], writes=[qh], sbuf=qh)
                k.dma("sp", [(kh.a, KT[h])], writes=[kh], sbuf=kh)
                k.cp("pool", va.a[:, :, 0:64], Vfull[:, :, h, :], reads=Vc, writes=[va])
            ps2 = pss[i % NB]
            pt = pts[i % NB]
            diag = (2 * kp >= 4 * g)
            los = []
            for u in range(2):
                kb = 2 * kp + u
                lo = max(0, kb - 4 * g) * 128
                los.append(lo)
                k.mm(ps2.a[:, u * T + lo:(u + 1) * T], kh.a[:, kb * 128:(kb + 1) * 128], qh.a[:, g * T + lo:(g + 1) * T],
                     True, True, reads=[kh, qh], writes=[ps2], sig=(u == 1))
            if not diag:
                k.act(pt.a, ps2.a, AF.Exp, reads=[ps2], writes=[pt])
            else:
                for u in range(2):
                    lo = los[u]
                    k.act(pt.a[:, u * T + lo:(u + 1) * T], ps2.a[:, u * T + lo:(u + 1) * T], AF.Exp, reads=[ps2], writes=[pt])
                    k.tt("pool", pt.a[:, u * T + lo:u * T + lo + 128], pt.a[:, u * T + lo:u * T + lo + 128], atri.a, ALU.mult,
                         reads=[pt, atri], writes=[pt])

        def back(i):
            h, g, kp = iters[i]
            va = vas[h % 2]
            nkb = 4 * g + 4
            pt = pts[i % NB]
            gi = h * NT + g
            acc = accs[gi % 2]
            for u in range(2):
                kb = 2 * kp + u
                lo = max(0, kb - 4 * g) * 128
                k.mm(acc.a[:, lo:T], va.a[:, kb, :], pt.a[:, u * T + lo:(u + 1) * T], kb == 0, kb == nkb - 1,
                     reads=[va, pt], writes=[acc], sig=(u == 1))
            if kp == 2 * g + 1:
                sl = slice(g * T, (g + 1) * T)
                rd = rden[gi % 2]
                yb = ybs[gi % 2]
                k.op("dve", lambda e: e.reciprocal(out=rd.a, in_=acc.a[64:128, :]), reads=[acc], writes=[rd])
                k.tt("dve", yb.a, acc.a[0:64, :], rd.a, ALU.mult, reads=[acc, rd], writes=[yb])
                k.dma("sp", [(YT[512 + h * 64:512 + (h + 1) * 64, sl], yb.a)], reads=[yb], sbuf=yb)

        for i in range(N + LA):
            if i < N:
                front(i)
            if i >= LA:
                back(i - LA)


def odd_stage1(k, G, L, j, x_src, YT):
    with k.stage():
        k.ps_alloc(8)
        xts = [k.sbc(8, [T], F32) for _ in range(1)]
        _, sqc = k.sbc(8, [T], BF16)
        hcs = [k.sbc(8, [T], BF16)[1] for _ in range(2)]
        rs = k.sb([128, T], F32)
        rso = k.sb([128, T], F32)
        _, wBc = k.sbc(16, [1024], BF16)
        wgl = k.sb([128, 128], BF16)
        wvA = k.sb([128, 8, 1024], BF16)
        wkA = k.sb([128, 8, 512], BF16)
        wg2 = k.sb([16, 512], BF16)
        bg2 = k.sb([128, 512], F32)
        k.dma("pool", [(wgl.a, G["od_wgl"][j])], writes=[wgl], sbuf=wgl)
        k.dma("pool", [(wg2.a, G["od_wg2"][j])], writes=[wg2], sbuf=wg2)
        k.dma("sp", [(bg2.a, G["od_bg2"][j].partition_broadcast(128))], writes=[bg2], sbuf=bg2)
        k.dma("pool", [(wvA.a, G["od_wvA"][j].rearrange("p (c n) -> p c n", n=1024))], writes=[wvA], sbuf=wvA)
        for oc in list(range(8, 16)) + list(range(0, 8)):
            k.dma("pool", [(wBc[oc].a, G["od_winB"][j, oc])], writes=[wBc[oc]], sbuf=wBc[oc])
        k.dma("pool", [(wkA.a, G["od_wkA"][j].rearrange("p (c n) -> p c n", n=512))], writes=[wkA], sbuf=wkA)
        glT = k.sb([16, T], BF16)
        spc = k.sbc(4, [512], F32)[1]
        Epf, Epc = k.sbc(4, [4, 128], F32)
        Enf, Enc = k.sbc(4, [4, 128], F32)
        El = k.sbc(4, [512], F32)[1]
        qe = k.sbc(4, [T], BF16)[1]
        ke = k.sbc(4, [T], BF16)[1]
        klc = k.sbc(4, [512], BF16)[1]
        vbc = k.sbc(4, [1024], BF16)[1]
        ats = [k.sb([128, 128], BF16) for _ in range(3)]
        Sst = [k.sb([128, 256], F32) for _ in range(4)]
        Sbf = [[k.sb([128, 256], BF16) for _ in range(2)] for _ in range(4)]
        sbi = [0, 0, 0, 0]
        for h in range(4):
            k.op("pool", lambda e: e.memset(Sst[h].a, 0.0), writes=[Sst[h]])
            k.op("pool", lambda e: e.memset(Sbf[h][0].a, 0.0), writes=[Sbf[h][0]])
        oT = k.sbc(8, [T], F32)[1]
        sgs = k.sbc(8, [T], BF16)[1]
        yts = [k.sbc(8, [T], BF16) for _ in range(1)]
        ctri, csup, cmbd = G["ctri"], G["csup"], G["cmbd"]
        gmx = G["gmix"]
        h0 = k.sb([128, 8, 2], F32)
        w32 = [k.sb([128, 1024], F32) for _ in range(1)]
        s00 = k.sb([2, 4], F32)

        def pro_load(t):
            sl = slice(t * T, (t + 1) * T)
            xt, xc = xts[0]
            k.dma("sp", [(xt, cview(x_src)[:, :, sl])], writes=xc, sbuf=xc[0])

        def pro_a(t):
            xt, xc = xts[0]
            norm_a(k, 8, [b.a for b in xc], xc, sqc)

        def pro_b(t):
            xt, xc = xts[0]
            norm_b(k, 8, [b.a for b in xc], xc, lambda c: gmx.a[:, L * 8 + c:L * 8 + c + 1], D,
                   sqc, hcs[t % 2], rs, G["ones"], gmx)

        def pro_b_stats(t):
            norm_stats(k, 8, sqc, rs, G["ones"], 1.0 / D, EPS)

        def pro_b_apply(t, c):
            xt, xc = xts[0]
            k.stt("dve", hcs[t % 2][c].a, xc[c].a, gmx.a[:, L * 8 + c:L * 8 + c + 1], rs.a, ALU.mult, ALU.mult,
                  reads=[xc[c], rs, gmx], writes=[hcs[t % 2][c]])

        pro_load(0)
        pro_a(0)
        pro_b(0)
        xt0, xc0 = xts[0]
        for tk in range(2):
            k.stt("dve", h0.a[:, :, tk], xt0[:, :, tk], rs.a[:, tk:tk + 1], gmx.a[:, L * 8:(L + 1) * 8], ALU.mult, ALU.mult,
                  reads=xc0 + [rs, gmx], writes=[h0])
        pq0 = k.ps()
        pk0 = k.ps()
        for oc in range(8):
            wb = w32[0]
            k.dma("sp", [(wb.a, G["od_winB"][j, oc])], writes=[wb], sbuf=wb)
            dstp = pq0 if oc < 4 else pk0
            for kc in range(8):
                k.mm(dstp.a[0:2, (oc % 4) * 128:(oc % 4 + 1) * 128], h0.a[:, kc, :], wb.a[:, kc * 128:(kc + 1) * 128],
                     kc == 0, kc == 7, reads=[h0, wb], writes=[dstp])
        k.cp("act", El[0].a[0:2, :], pk0.a[0:2, :], reads=[pk0], writes=[El[0]])
        k.tt("dve", rso.a[0:2, :], pq0.a[0:2, :], El[0].a[0:2, :], ALU.mult, reads=[pq0, El[0]], writes=[rso])
        k.op("dve", lambda e: e.reduce_sum(out=s00.a, in_=rso.a[0:2, :].rearrange("p (h d) -> p h d", d=128), axis=mybir.AxisListType.X),
             reads=[rso], writes=[s00])
        k.ts("dve", s00.a, s00.a, 128.0 ** -0.5, None, ALU.mult, None, reads=[s00], writes=[s00])
        for t in range(NT):
            sl = slice(t * T, (t + 1) * T)
            hc = hcs[t % 2]
            if t + 1 < NT:
                pro_load(t + 1)

            def projB(oc_idx):
                pb = k.ps()
                for kc in range(8):
                    k.mm(pb.a, wBc[oc_idx].a[:, kc * 128:(kc + 1) * 128], hc[kc].a, kc == 0, kc == 7,
                         reads=[wBc[oc_idx], hc[kc]], writes=[pb])
                return pb

            pb = k.ps()
            for kc in range(8):
                k.mm(pb.a[0:16, :], wgl.a[:, kc * 16:(kc + 1) * 16], hc[kc].a, kc == 0, kc == 7,
                     reads=[wgl, hc[kc]], writes=[pb])
            k.cp("act", glT.a, pb.a[0:16, :], reads=[pb], writes=[glT])
            for b in range(4):
                bs = slice(b * 128, (b + 1) * 128)
                pb = k.ps()
                k.mm(pb.a, glT.a[:, bs], wg2.a, True, True, reads=[glT, wg2], writes=[pb])
                pr = spc[b]
                k.tt("dve", pr.a, pb.a, bg2.a, ALU.add, reads=[pb, bg2], writes=[pr])
                k.act(pr.a, pr.a, AF.Exp, reads=[pr], writes=[pr], scale=-1.0)
                k.act(pr.a, pr.a, AF.Ln, reads=[pr], writes=[pr], bias=1.0)
            for b in range(4):
                bs = slice(b * 128, (b + 1) * 128)
                for half in range(2):
                    pb = k.ps()
                    for kc in range(8):
                        k.mm(pb.a, hc[kc].a[:, bs], wvA.a[:, kc, half * 512:(half + 1) * 512], kc == 0, kc == 7,
                             reads=[wvA, hc[kc]], writes=[pb])
                    k.cp("act", vbc[b].a[:, half * 512:(half + 1) * 512], pb.a, reads=[pb], writes=[vbc[b]])
            for b in range(4):
                pb = k.ps()
                for h in range(4):
                    k.mm(pb.a[:, h * 128:(h + 1) * 128], spc[b].a[:, h * 128:(h + 1) * 128], ctri.a, True, True,
                         reads=[spc[b], ctri], writes=[pb], sig=(h == 3))
                pv4 = pb.a.rearrange("p (h s) -> p h s", s=128)
                k.act(Epc[b].a, pv4, AF.Exp, reads=[pb], writes=[Epc[b]])
                k.act(Enc[b].a, pv4, AF.Exp, reads=[pb], writes=[Enc[b]], scale=-1.0)
                pb = k.ps()
                k.mm(pb.a, csup.a, spc[b].a, True, True, reads=[csup, spc[b]], writes=[pb])
                k.act(El[b].a, pb.a, AF.Exp, reads=[pb], writes=[El[b]])
            for ci in range(8):
                pg = projB(8 + ci)
                k.act(sgs[ci].a, pg.a, AF.Silu, reads=[pg], writes=[sgs[ci]])
            if t + 1 < NT:
                pro_a(t + 1)
            for h in range(4):
                pq = projB(h)
                k.stt("dve", qe[h].a.rearrange("p (b s) -> p b s", s=128), pq.a.rearrange("p (b s) -> p b s", s=128),
                      128.0 ** -0.5, Epf[:, :, h, :], ALU.mult, ALU.mult, reads=[pq] + Epc, writes=[qe[h]])
                pk = projB(4 + h)
                k.tt("dve", ke[h].a.rearrange("p (b s) -> p b s", s=128), pk.a.rearrange("p (b s) -> p b s", s=128),
                     Enf[:, :, h, :], ALU.mult, reads=[pk] + Enc, writes=[ke[h]])
            for b in range(4):
                bs = slice(b * 128, (b + 1) * 128)
                pb = k.ps()
                for kc in range(8):
                    k.mm(pb.a, hc[kc].a[:, bs], wkA.a[:, kc, :], kc == 0, kc == 7, reads=[wkA, hc[kc]], writes=[pb])
                k.tt("dve", klc[b].a, pb.a, El[b].a, ALU.mult, reads=[pb, El[b]], writes=[klc[b]])
            if t + 1 < NT:
                pro_b_stats(t + 1)
            ai = 0
            for b in range(4):
                bs = slice(b * 128, (b + 1) * 128)
                for h in range(4):
                    pds = []
                    for c in range(2):
                        rows = slice(c * 64, (c + 1) * 64)
                        p_ = k.ps()
                        k.mm(p_.a[:, 0:256], klc[b].a[rows, h * 128:(h + 1) * 128], vbc[b].a[rows, h * 256:(h + 1) * 256],
                             True, True, reads=[klc[b], vbc[b]], writes=[p_])
                        pds.append(p_)
                    sb0 = Sbf[h][sbi[h] % 2]
                    sb1 = Sbf[h][(sbi[h] + 1) % 2]
                    dec0 = Epf[:, b, h, 63:64]
                    dec1 = Epf[:, b, h, 127:128]
                    k.stt("dve", Sst[h].a, Sst[h].a, dec0, pds[0].a[:, 0:256], ALU.mult, ALU.add,
                          reads=[Sst[h], Epc[b], pds[0]], writes=[Sst[h]])
                    k.cp("act", sb1.a, Sst[h].a, reads=[Sst[h]], writes=[sb1])
                    pat = k.ps()
                    k.mm(pat.a[:, 0:128], ke[h].a[:, bs], qe[h].a[:, bs], True, True, reads=[ke[h], qe[h]], writes=[pat])
                    at = ats[ai % 3]
                    ai += 1
                    k.tt("dve", at.a, pat.a[:, 0:128], cmbd.a, ALU.mult, reads=[pat, cmbd], writes=[at])
                    if t == 0 and b == 0:
                        k.cp("dve", at.a[0:1, 0:1], s00.a[0:1, h:h + 1], reads=[s00, at], writes=[at])
                    po = [k.ps(), k.ps()]
                    for dvc in range(2):
                        k.mm(po[dvc].a[:, 0:64], sb0.a[:, dvc * 128:(dvc + 1) * 128], qe[h].a[:, b * 128:b * 128 + 64],
                             True, False, reads=[sb0, qe[h]], writes=[po[dvc]], sig=False)
                    for dvc in range(2):
                        k.mm(po[dvc].a[:, 64:128], sb1.a[:, dvc * 128:(dvc + 1) * 128], qe[h].a[:, b * 128 + 64:b * 128 + 128],
                             False, False, reads=[sb1, qe[h]], writes=[po[dvc]], sig=False)
                    for dvc in range(2):
                        k.mm(po[dvc].a[:, 0:128], vbc[b].a[:, h * 256 + dvc * 128:h * 256 + (dvc + 1) * 128], at.a,
                             False, True, reads=[vbc[b], at], writes=[po[dvc]], sig=True)
                    k.stt("dve", Sst[h].a, Sst[h].a, dec1, pds[1].a[:, 0:256], ALU.mult, ALU.add,
                          reads=[Sst[h], Epc[b], pds[1]], writes=[Sst[h]])
                    k.cp("act", sb0.a, Sst[h].a, reads=[Sst[h]], writes=[sb0])
                    for dvc in range(2):
                        k.cp("act", oT[h * 2 + dvc].a[:, bs], po[dvc].a[:, 0:128], reads=[po[dvc]], writes=[oT[h * 2 + dvc]])
                    if t + 1 < NT and h % 2 == 1:
                        pro_b_apply(t + 1, b * 2 + h // 2)
            yt, yc = yts[0]
            for h in range(4):
                srcs = [oT[h * 2], oT[h * 2 + 1]]
                norm_a(k, 2, [b_.a for b_ in srcs], srcs, sqc)
                norm_b(k, 2, [b_.a for b_ in srcs], srcs, lambda c: G["go"].a[:, j * 2 + c:j * 2 + c + 1], 256,
                       sqc, srcs, rso, G["ones"], G["go"])
                for dvc in range(2):
                    ci = h * 2 + dvc
                    k.tt("dve", yc[ci].a, oT[ci].a, sgs[ci].a, ALU.mult, reads=[oT[ci], sgs[ci]], writes=[yc[ci]])
            k.dma("sp", [(cview(YT)[:, :, sl], yt)], reads=yc, sbuf=yc[0])


def build_program(nlayers=4):
    nc = bass.Bass("TRN2", target_bir_lowering=False)

    def din(name, shape, dtype=F32):
        return nc.dram_tensor(name, list(shape), dtype, kind="ExternalInput").ap()

    xT = din("xT", [D, S])
    G = {}
    G["pos"] = din("pos", [1, S], I32)
    smalls = {"gmix": 32, "gmlp": 32, "gfin": 8, "gq": 6, "gkv": 4, "go": 4, "cw": 24, "crope": 2}
    small_in = {n: din(n, [128, w]) for n, w in smalls.items()}
    consts_in = {n: din(n, [128, 128]) for n in ("ctri", "csup", "cmbd", "atri")}
    G["ev_win"] = din("ev_win", [2, 19, 128, 1024])
    G["ev_wq"] = din("ev_wq", [2, 8, 128, 288])
    G["ev_wqs"] = din("ev_wqs", [2, 8, 128, 288])
    G["ev_wkk"] = din("ev_wkk", [2, 8, 128, 128])
    G["ev_wv"] = din("ev_wv", [2, 128, 1024])
    G["ev_wo"] = din("ev_wo", [2, 8, 128, 1024])
    G["od_winB"] = din("od_winB", [2, 16, 128, 1024])
    G["od_wgl"] = din("od_wgl", [2, 128, 128])
    G["od_wvA"] = din("od_wvA", [2, 128, 8192])
    G["od_wkA"] = din("od_wkA", [2, 128, 4096])
    G["od_wg2"] = din("od_wg2", [2, 16, 512])
    G["od_bg2"] = din("od_bg2", [2, 1, 512])
    G["od_wo"] = din("od_wo", [2, 8, 128, 1024])
    G["w1"] = din("w1", [4, 32, 128, 1024])
    G["w2"] = din("w2", [4, 8, 128, 4096])
    outT = nc.dram_tensor("outT", [D, S], F32, kind="ExternalOutput").ap()
    xs = nc.dram_tensor("xs", [D, S], F32).ap()
    YT = nc.dram_tensor("YTs", [D, S], BF16).ap()
    QT = nc.dram_tensor("QTs", [8, 96, S], BF16).ap()
    KT = nc.dram_tensor("KTs", [8, 96, S], BF16).ap()

    k = KB(nc)
    with k.es:
        k.es_root = k.es
        for n, w in smalls.items():
            b = k.sb([128, w], F32)
            k.dma("sp", [(b.a, small_in[n])], writes=[b], sbuf=b)
            G[n] = b
        for n in ("ctri", "csup", "cmbd"):
            b = k.sb([128, 128], F32)
            k.dma("sp", [(b.a, consts_in[n])], writes=[b], sbuf=b)
            G[n] = b
        b = k.sb([128, 128], BF16)
        k.dma("pool", [(b.a, consts_in["atri"])], writes=[b], sbuf=b)
        G["atri"] = b
        ones = k.sb([128, 128], BF16)
        k.op("pool", lambda e: e.memset(ones.a, 1.0), writes=[ones])
        G["ones"] = ones
        hpi = k.sb([128, 1], F32)
        k.op("pool", lambda e: e.memset(hpi.a, math.pi / 2), writes=[hpi])
        G["hpi"] = hpi
        G["negpi"] = hpi
        k.barrier()
        x_cur = xT
        for L in range(nlayers):
            j = L // 2
            last = (L == nlayers - 1)
            if L % 2 == 0:
                with k.stage():
                    vt = k.es.enter_context(nc.sbuf_tensor(k._name("Vres"), [128, 32, 8, 64], BF16))
                    Vfull = vt[:]
                    Vc = [Buf(Vfull[:, t * 4:(t + 1) * 4]) for t in range(NT)]
                    sbufs = k.stage_bufs
                    even_stage1(k, G, L, j, x_cur, YT, QT, KT, Vfull, Vc)
                    k.stage_bufs = sbufs
                    even_stage2(k, G, YT, QT, KT, Vfull, Vc)
                    k.stage_bufs = sbufs
                mlp_stage(k, G, L, x_cur, xs, G["ev_wo"][j], YT, last, outT)
            else:
                odd_stage1(k, G, L, j, x_cur, YT)
                mlp_stage(k, G, L, x_cur, xs, G["od_wo"][j], YT, last, outT)
            x_cur = xs
        k.barrier()
    return nc


def _bform(w, m=128):
    K, N = w.shape
    kc, oc = K // 128, N // m
    return np.ascontiguousarray(w.reshape(kc, 128, oc, m).transpose(2, 1, 0, 3).reshape(oc, 128, kc * m))


def _aform(w):
    K, N = w.shape
    kc = K // 128
    return np.ascontiguousarray(w.reshape(kc, 128, N).transpose(1, 0, 2).reshape(128, kc * N))


def _pvec(g):
    g2 = g.reshape(-1, g.shape[-1] // 128, 128)
    return np.ascontiguousarray(g2.transpose(2, 0, 1).reshape(128, -1))


def prepare_shared(inp):
    f = lambda a: np.asarray(a, dtype=np.float32)
    sh = {}
    sh["gmix"] = _pvec(f(inp["mix_norm_g"]))
    sh["gmlp"] = _pvec(f(inp["mlp_norm_g"]))
    sh["gfin"] = _pvec(f(inp["final_norm_g"])[None])
    sh["gq"] = _pvec(f(inp["ev_q_norm_g"]))
    sh["gkv"] = _pvec(f(inp["ev_kv_norm_g"]))
    sh["go"] = _pvec(f(inp["od_o_norm_g"]))
    cw = f(inp["ev_conv_w"])
    sh["cw"] = np.ascontiguousarray(cw.reshape(2, 3, 4, 128).transpose(3, 0, 2, 1).reshape(128, 24))
    inv_freq = (1.0 / (10000.0 ** (np.arange(0, 32, 2, dtype=np.float32) / np.float32(32)))).astype(np.float32)
    crope = np.zeros((128, 2), np.float32)
    crope[64:80, 0] = inv_freq
    crope[80:96, 0] = inv_freq
    crope[64:80, 1] = -1.0
    crope[80:96, 1] = 1.0
    sh["crope"] = crope
    jj = np.arange(128)[:, None]
    ii = np.arange(128)[None, :]
    same = (jj // 64) == (ii // 64)
    sh["ctri"] = np.where(same & (jj <= ii), -1.0 / 16.0, 0.0).astype(np.float32)
    sh["csup"] = np.where(same & (jj > ii), -1.0 / 16.0, 0.0).astype(np.float32)
    sh["cmbd"] = np.where(same & (jj <= ii), 1.0, 0.0).astype(np.float32)
    sh["atri"] = np.where(jj <= ii, 1.0, 0.0).astype(np.float32)
    ev_win, ev_wq, ev_wqs, ev_wkk, ev_wv, ev_wo = [], [], [], [], [], []
    for j in range(2):
        w = f(inp["ev_w_in"][j])
        pe = w[:, 2176:2208]
        pesw = np.concatenate([pe[:, 16:32], pe[:, 0:16]], axis=1)
        wext = np.concatenate([w[:, :2176], pe, pe, pe, pe[:, 0:32], pe, pe, pesw, pe[:, 0:32]], axis=1)
        assert wext.shape[1] == 19 * 128
        ev_win.append(_bform(wext))
        wq = f(inp["ev_w_qb"][j]).reshape(384, 8, 96)
        wqs = np.concatenate([wq[:, :, 0:64], wq[:, :, 80:96], wq[:, :, 64:80]], axis=2)
        ev_wq.append(_bform(wq.reshape(384, 768), 96))
        ev_wqs.append(_bform(wqs.reshape(384, 768), 96))
        wkv = f(inp["ev_w_kvb"][j]).reshape(256, 8, 128)
        ev_wkk.append(_bform(np.ascontiguousarray(wkv[:, :, 0:64]).reshape(256, 512), 64))
        ev_wv.append(_aform(np.ascontiguousarray(wkv[:, :, 64:128]).reshape(256, 512)))
        ev_wo.append(_bform(f(inp["ev_w_out"][j])))
    sh["ev_win"] = np.stack(ev_win)
    sh["ev_wq"] = np.stack(ev_wq)
    sh["ev_wqs"] = np.stack(ev_wqs)
    sh["ev_wkk"] = np.stack(ev_wkk)
    sh["ev_wv"] = np.stack(ev_wv)
    sh["ev_wo"] = np.stack(ev_wo)
    od_winB, od_wgl, od_wvA, od_wkA, od_wo = [], [], [], [], []
    for j in range(2):
        w = f(inp["od_w_in"][j])
        od_winB.append(_bform(np.concatenate([w[:, 0:1024], w[:, 2048:3072]], axis=1)))
        od_wgl.append(_bform(w[:, 3072:3088], 16)[0])
        od_wvA.append(_aform(w[:, 1024:2048]))
        od_wkA.append(_aform(w[:, 512:1024]))
        od_wo.append(_bform(f(inp["od_w_out"][j])))
    sh["od_winB"] = np.stack(od_winB)
    sh["od_wgl"] = np.stack(od_wgl)
    sh["od_wvA"] = np.stack(od_wvA)
    sh["od_wkA"] = np.stack(od_wkA)
    sh["od_wg2"] = np.ascontiguousarray(f(inp["od_w_gate2"]))
    sh["od_bg2"] = np.ascontiguousarray(f(inp["od_b_gate2"]).reshape(2, 1, 512))
    sh["od_wo"] = np.stack(od_wo)
    sh["w1"] = np.stack([_bform(f(inp["mlp_w1"][l])) for l in range(4)])
    sh["w2"] = np.stack([_bform(f(inp["mlp_w2"][l])) for l in range(4)])
    return sh


def run(inp, nlayers=4, ncores=8):
    sh = prepare_shared(inp)
    x = np.asarray(inp["x"], dtype=np.float32)
    pos = np.asarray(inp["positions"], dtype=np.int32)
    nc = build_program(nlayers)
    in_maps = []
    for b in range(ncores):
        m = dict(sh)
        m["xT"] = np.ascontiguousarray(x[b].T)
        m["pos"] = np.ascontiguousarray(pos[b][None, :])
        in_maps.append(m)
    res = run_bass_kernel_spmd(nc, in_maps, core_ids=list(range(ncores)))
    return np.stack([np.ascontiguousarray(r["outT"].T) for r in res.results], axis=0)


def kernel(**inputs):
    return run(inputs, 4, 8).astype(np.float32)
```

```python
import math
from contextlib import ExitStack, contextmanager
import numpy as np
import concourse.bass as bass
import concourse.mybir as mybir
from concourse.bass_utils import run_bass_kernel_spmd

F32, BF16, I32 = mybir.dt.float32, mybir.dt.bfloat16, mybir.dt.int32
AF = mybir.ActivationFunctionType
ALU = mybir.AluOpType

S = 4096
D = 1024
T = 512
NT = S // T
EPS = 1e-6
TWO_PI = 2.0 * math.pi


class Buf:
    __slots__ = ("a", "w", "r", "ds")

    def __init__(self, a):
        self.a = a
        self.w = None
        self.r = {}
        self.ds = None


class KB:
    def __init__(self, nc):
        self.nc = nc
        self.es = ExitStack()
        self.eng = {"pe": nc.tensor, "act": nc.scalar, "dve": nc.vector, "pool": nc.gpsimd, "sp": nc.sync}
        self.sem = {}
        self.cnt = {}
        self.seen = {e: {} for e in self.eng}
        self.pend = {e: ([], []) for e in self.eng}
        for e in ("pe", "act", "dve", "pool"):
            self.sem[e] = self.es.enter_context(nc.semaphore("c_" + e))
            self.cnt[e] = 0
        self.dsems = []
        self.free_ds = {}
        self.stage_bufs = None
        self.uid = 0
        self.psi = 0
        self.psb = []

    def _name(self, p):
        self.uid += 1
        return f"{p}{self.uid}"

    def sb(self, shape, dtype):
        t = self.es.enter_context(self.nc.sbuf_tensor(self._name("sb"), list(shape), dtype))
        b = Buf(t[:])
        if self.stage_bufs is not None:
            self.stage_bufs.append(b)
        return b

    def sbc(self, n, inner, dtype, parts=128):
        t = self.es.enter_context(self.nc.sbuf_tensor(self._name("sbc"), [parts, n] + list(inner), dtype))
        full = t[:]
        ch = [Buf(full[:, c]) for c in range(n)]
        if self.stage_bufs is not None:
            self.stage_bufs.extend(ch)
        return full, ch

    def ps_alloc(self, n):
        self.psb = []
        for _ in range(n):
            t = self.es.enter_context(self.nc.psum_tensor(self._name("ps"), [128, 512], F32))
            self.psb.append(Buf(t[:]))
        self.psi = 0

    def ps(self):
        b = self.psb[self.psi % len(self.psb)]
        self.psi += 1
        return b

    def _ds(self, b, q):
        kind = "sw" if q == "pool" else "hw"
        if b.ds is None:
            b.ds = {}
        if kind not in b.ds:
            fl = self.free_ds.setdefault(kind, [])
            if fl:
                b.ds[kind] = fl.pop()
            else:
                sm = self.es_root.enter_context(self.nc.semaphore(self._name("d" + kind)))
                d = [sm, 0]
                b.ds[kind] = d
                self.dsems.append(d)
        return b.ds[kind]

    def _wait(self, e, toks):
        en = self.eng[e]
        seen = self.seen[e]
        for tk in toks:
            if tk is None:
                continue
            sm, v = tk
            if seen.get(sm, 0) >= v:
                continue
            seen[sm] = v
            en.wait_ge(sm, v)

    @staticmethod
    def _deps(reads, writes):
        d = []
        for b in reads:
            d.append(b.w)
        for b in writes:
            d.append(b.w)
            d.extend(b.r.values())
        return d

    def op(self, e, fn, reads=(), writes=(), sig=True):
        self._wait(e, self._deps(reads, writes))
        ins = fn(self.eng[e])
        pr, pw = self.pend[e]
        pr.extend(reads)
        pw.extend(writes)
        if sig:
            self.cnt[e] += 1
            sm = self.sem[e]
            tk = (sm, self.cnt[e])
            ins.then_inc(sm, 1)
            for b in pr:
                b.r[sm] = tk
            for b in pw:
                b.w = tk
                b.r = {}
            pr.clear()
            pw.clear()
            return tk
        return None

    def dma(self, q, pairs, reads=(), writes=(), sbuf=None):
        self._wait(q, self._deps(reads, writes))
        ds = self._ds(sbuf, q)
        for (o, i) in pairs:
            self.eng[q].dma_start(out=o, in_=i).then_inc(ds[0], 16)
            ds[1] += 16
        tk = (ds[0], ds[1])
        for b in reads:
            b.r[ds[0]] = tk
        for b in writes:
            b.w = tk
            b.r = {}
        return tk

    def barrier(self):
        for e in self.eng:
            assert not self.pend[e][0] and not self.pend[e][1], e
        toks = [(self.sem[e], self.cnt[e]) for e in self.sem if self.cnt[e] > 0]
        toks += [(d[0], d[1]) for d in self.dsems if d[1] > 0]
        for e in self.eng:
            self._wait(e, toks)

    @contextmanager
    def stage(self):
        outer = self.es
        self.es = ExitStack()
        self.stage_bufs = []
        try:
            yield
        finally:
            self.barrier()
            for b in self.stage_bufs:
                if b.ds is not None:
                    for kind, d in b.ds.items():
                        self.free_ds.setdefault(kind, []).append(d)
                    b.ds = None
            self.stage_bufs = None
            self.es.close()
            self.es = outer

    def mm(self, out, lhsT, rhs, start, stop, reads, writes, sig=None):
        if sig is None:
            sig = stop
        return self.op("pe", lambda e: e.matmul(out, lhsT=lhsT, rhs=rhs, start=start, stop=stop),
                       reads=reads, writes=writes, sig=sig)

    def act(self, out, in_, func, reads, writes, **kw):
        return self.op("act", lambda e: e.activation(out=out, in_=in_, func=func, **kw), reads=reads, writes=writes)

    def tt(self, e, out, in0, in1, op, reads, writes):
        return self.op(e, lambda en: en.tensor_tensor(out=out, in0=in0, in1=in1, op=op), reads=reads, writes=writes)

    def stt(self, e, out, in0, scalar, in1, op0, op1, reads, writes):
        return self.op(e, lambda en: en.scalar_tensor_tensor(out=out, in0=in0, scalar=scalar, in1=in1, op0=op0, op1=op1),
                       reads=reads, writes=writes)

    def ts(self, e, out, in0, s1, s2, op0, op1, reads, writes):
        if s2 is None:
            return self.op(e, lambda en: en.tensor_scalar(out=out, in0=in0, scalar1=s1, scalar2=None, op0=op0),
                           reads=reads, writes=writes)
        return self.op(e, lambda en: en.tensor_scalar(out=out, in0=in0, scalar1=s1, scalar2=s2, op0=op0, op1=op1),
                       reads=reads, writes=writes)

    def cp(self, e, out, in_, reads, writes):
        if e == "act":
            return self.op(e, lambda en: en.copy(out=out, in_=in_), reads=reads, writes=writes)
        return self.op(e, lambda en: en.tensor_copy(out=out, in_=in_), reads=reads, writes=writes)


def cview(ap):
    return ap.rearrange("(c p) s -> p c s", p=128)


def norm_a(k, C, src_aps, src_bufs, sqc):
    for c in range(C):
        k.act(sqc[c].a, src_aps[c], AF.Square, reads=[src_bufs[c]], writes=[sqc[c]])


def norm_b(k, C, src_aps, src_bufs, gain, Dn, sqc, outc, rs, ones, gbuf, in_scale=None, bias=None, eng="dve"):
    if in_scale is None:
        in_scale = 1.0 / Dn
    if bias is None:
        bias = EPS
    norm_stats(k, C, sqc, rs, ones, in_scale, bias)
    for c in range(C):
        k.stt(eng, outc[c].a, src_aps[c], gain(c), rs.a, ALU.mult, ALU.mult,
              reads=[src_bufs[c], rs, gbuf], writes=[outc[c]])


def norm_stats(k, C, sqc, rs, ones, in_scale, bias):
    pb = k.ps()
    for c in range(C):
        k.mm(pb.a, ones.a, sqc[c].a, c == 0, c == C - 1, reads=[ones, sqc[c]], writes=[pb])
    k.act(rs.a, pb.a, AF.Ln, reads=[pb], writes=[rs], scale=in_scale, bias=bias)
    k.act(rs.a, rs.a, AF.Exp, reads=[rs], writes=[rs], scale=-0.5)


def rmsnorm_fm(k, C, src_aps, src_bufs, gain, Dn, sqc, outc, rs, ones, gbuf, in_scale=None, bias=None):
    norm_a(k, C, src_aps, src_bufs, sqc)
    norm_b(k, C, src_aps, src_bufs, gain, Dn, sqc, outc, rs, ones, gbuf, in_scale, bias)


def mlp_stage(k, G, L, x_src, x_dst, mix_w, YT, final, outT):
    w1, w2 = G["w1"], G["w2"]
    with k.stage():
        k.ps_alloc(8)
        xts = [k.sbc(8, [T], F32) for _ in range(3)]
        yts = [k.sbc(8, [T], BF16) for _ in range(2)]
        _, woc = k.sbc(8, [1024], BF16)
        for oc in range(8):
            k.dma("pool", [(woc[oc].a, mix_w[oc])], writes=[woc[oc]], sbuf=woc[oc])
        _, sqc = k.sbc(8, [T], BF16)
        hcs = [k.sbc(8, [T], BF16)[1] for _ in range(2)]
        _, hidc = k.sbc(32, [T], BF16)
        rs = k.sb([128, T], F32)
        rl = [k.sb([128, T], BF16) for _ in range(3)]
        w1b = [k.sb([128, 4, 1024], BF16) for _ in range(3)]
        w2b = [k.sb([128, 4096], BF16) for _ in range(3)]
        gm = G["gmlp"]

        def pro_load(t):
            sl = slice(t * T, (t + 1) * T)
            xt, xc = xts[t % 3]
            yt, yc = yts[t % 2]
            k.dma("sp", [(xt, cview(x_src)[:, :, sl])], writes=xc, sbuf=xc[0])
            k.dma("sp", [(yt, cview(YT)[:, :, sl])], writes=yc, sbuf=yc[0])

        def pro_a(t):
            xt, xc = xts[t % 3]
            yt, yc = yts[t % 2]
            for oc in range(8):
                pb = k.ps()
                for kc in range(8):
                    k.mm(pb.a, woc[oc].a[:, kc * 128:(kc + 1) * 128], yc[kc].a, kc == 0, kc == 7,
                         reads=[woc[oc], yc[kc]], writes=[pb])
                k.tt("dve", xc[oc].a, pb.a, xc[oc].a, ALU.add, reads=[pb, xc[oc]], writes=[xc[oc]])
            norm_a(k, 8, [b.a for b in xc], xc, sqc)

        def pro_b(t):
            xt, xc = xts[t % 3]
            norm_b(k, 8, [b.a for b in xc], xc, lambda c: gm.a[:, L * 8 + c:L * 8 + c + 1], D,
                   sqc, hcs[t % 2], rs, G["ones"], gm)

        pro_load(0)
        pro_a(0)
        pro_b(0)
        wi = 0
        for t in range(NT):
            sl = slice(t * T, (t + 1) * T)
            xt, xc = xts[t % 3]
            hc = hcs[t % 2]
            if t + 1 < NT:
                pro_load(t + 1)
            for og in range(8):
                wb = w1b[wi % 3]
                wi += 1
                k.dma("pool", [(wb.a, w1[L, og * 4:(og + 1) * 4].rearrange("o p n -> p o n"))], writes=[wb], sbuf=wb)
                for o4 in range(4):
                    oc = og * 4 + o4
                    pb = k.ps()
                    for kc in range(8):
                        k.mm(pb.a, wb.a[:, o4, kc * 128:(kc + 1) * 128], hc[kc].a, kc == 0, kc == 7,
                             reads=[wb, hc[kc]], writes=[pb])
                    r = rl[oc % 3]
                    k.act(r.a, pb.a, AF.Relu, reads=[pb], writes=[r])
                    k.tt("dve", hidc[oc].a, r.a, r.a, ALU.mult, reads=[r], writes=[hidc[oc]])
                if og == 5 and t + 1 < NT:
                    pro_a(t + 1)
            for oc in range(8):
                wb = w2b[(t * 8 + oc) % 3]
                k.dma("pool", [(wb.a, w2[L, oc])], writes=[wb], sbuf=wb)
                pb = k.ps()
                for kc in range(32):
                    k.mm(pb.a, wb.a[:, kc * 128:(kc + 1) * 128], hidc[kc].a, kc == 0, kc == 31,
                         reads=[wb, hidc[kc]], writes=[pb])
                k.tt("dve", xc[oc].a, pb.a, xc[oc].a, ALU.add, reads=[pb, xc[oc]], writes=[xc[oc]])
                if oc == 2 and t + 1 < NT:
                    pro_b(t + 1)
            if final:
                rmsnorm_fm(k, 8, [b.a for b in xc], xc, lambda c: G["gfin"].a[:, c:c + 1], D,
                           sqc, xc, rs, G["ones"], G["gfin"])
                k.dma("sp", [(cview(outT)[:, :, sl], xt)], reads=xc, sbuf=xc[0])
            else:
                k.dma("sp", [(cview(x_dst)[:, :, sl], xt)], reads=xc, sbuf=xc[0])


def even_stage1(k, G, L, j, x_src, YT, QT, KT, Vfull, Vc):
    pos = G["pos"]
    with k.stage():
        k.ps_alloc(8)
        xts = [k.sbc(8, [T], F32) for _ in range(2)]
        _, sqc = k.sbc(8, [T], BF16)
        _, sqz = k.sbc(3, [T], BF16)
        hcs = [k.sbc(8, [T], BF16)[1] for _ in range(2)]
        rs = k.sb([128, T], F32)
        rsz = k.sb([128, T], F32)
        _, winc = k.sbc(19, [1024], BF16)
        wq = k.sb([128, 8, 288], BF16)
        wqs = k.sb([128, 8, 288], BF16)
        wkk = k.sb([128, 8, 128], BF16)
        wv = k.sb([128, 1024], BF16)
        worder = [12, 13, 14, 0, 4, 8, 1, 5, 9, 15, 16, 2, 6, 10, 3, 7, 11, 17, 18]
        for n_, oc in enumerate(worder):
            k.dma("pool", [(winc[oc].a, G["ev_win"][j, oc])], writes=[winc[oc]], sbuf=winc[oc])
            if n_ == 8:
                for (dst, src) in ((wq, G["ev_wq"][j]), (wqs, G["ev_wqs"][j])):
                    k.dma("pool", [(dst.a, src.rearrange("o p n -> p o n"))], writes=[dst], sbuf=dst)
        k.dma("pool", [(wkk.a, G["ev_wkk"][j].rearrange("o p n -> p o n"))], writes=[wkk], sbuf=wkk)
        k.dma("pool", [(wv.a, G["ev_wv"][j])], writes=[wv], sbuf=wv)
        ufull, uc = k.sbc(4, [T + 2], F32)
        for c in range(4):
            k.op("pool", lambda e: e.memset(uc[c].a[:, 0:2], 0.0), writes=[uc[c]])
        acs = [k.sb([128, T], F32) for _ in range(1)]
        acc = [k.sb([128, T], F32) for _ in range(1)]
        yats = [k.sbc(4, [T], BF16) for _ in range(1)]
        zqn = k.sbc(3, [T], BF16)[1]
        zkvn = k.sbc(2, [T], BF16)[1]
        qt = k.sb([96, 8, T], BF16)
        kts = k.sb([96, 8, T], BF16)
        posi = k.sb([128, T], I32)
        posd = k.sb([128, T], I32)
        ang = k.sb([128, T], F32)
        rtmp = k.sb([128, T], F32)
        cosT = k.sb([128, T], F32)
        sinS = k.sb([128, T], F32)
        t1s = [k.sb([128, T], F32) for _ in range(2)]
        t2s = [k.sb([128, T], F32) for _ in range(2)]
        kpe = k.sb([128, T], BF16)
        crope = G["crope"]
        gmx = G["gmix"]
        cwb = G["cw"]
        R = slice(64, 96)

        def pro_load(t):
            sl = slice(t * T, (t + 1) * T)
            xt, xc = xts[t % 2]
            k.dma("sp", [(xt, cview(x_src)[:, :, sl])], writes=xc, sbuf=xc[0])

        def pro_a(t):
            xt, xc = xts[t % 2]
            norm_a(k, 8, [b.a for b in xc], xc, sqc)

        def pro_b(t):
            xt, xc = xts[t % 2]
            norm_b(k, 8, [b.a for b in xc], xc, lambda c: gmx.a[:, L * 8 + c:L * 8 + c + 1], D,
                   sqc, hcs[t % 2], rs, G["ones"], gmx)

        def rope_steps(t):
            sl = slice(t * T, (t + 1) * T)
            st = []
            st.append(lambda: k.dma("sp", [(posd.a[R, :], pos[0:1, sl].partition_broadcast(32))], writes=[posd], sbuf=posd))
            st.append(lambda: k.cp("dve", ang.a[R, :], posd.a[R, :], reads=[posd], writes=[ang]))
            st.append(lambda: k.ts("dve", ang.a[R, :], ang.a[R, :], crope.a[R, 0:1], None, ALU.mult, None, reads=[ang, crope], writes=[ang]))
            for (shift, dst) in ((0.0, sinS), (math.pi / 2, cosT)):
                st.append(lambda shift=shift: k.ts("dve", rtmp.a[R, :], ang.a[R, :], shift, 1.0 / TWO_PI, ALU.add, ALU.mult, reads=[ang], writes=[rtmp]))
                st.append(lambda: k.cp("dve", posi.a[R, :], rtmp.a[R, :], reads=[rtmp], writes=[posi]))
                st.append(lambda: k.cp("dve", rtmp.a[R, :], posi.a[R, :], reads=[posi], writes=[rtmp]))
                st.append(lambda: k.stt("dve", rtmp.a[R, :], rtmp.a[R, :], -TWO_PI, ang.a[R, :], ALU.mult, ALU.add, reads=[rtmp, ang], writes=[rtmp]))
                st.append(lambda shift=shift: k.ts("dve", rtmp.a[R, :], rtmp.a[R, :], -math.pi - shift, math.pi - shift, ALU.max, ALU.min, reads=[rtmp], writes=[rtmp]))
                if shift == 0.0:
                    st.append(lambda: k.act(rtmp.a[R, :], rtmp.a[R, :], AF.Sin, reads=[rtmp], writes=[rtmp]))
                    st.append(lambda dst=dst: k.ts("dve", dst.a[R, :], rtmp.a[R, :], crope.a[R, 1:2], None, ALU.mult, None, reads=[rtmp, crope], writes=[dst]))
                else:
                    st.append(lambda dst=dst: k.act(dst.a[R, :], rtmp.a[R, :], AF.Sin, reads=[rtmp, G["hpi"]], writes=[dst], bias=G["hpi"].a[R, 0:1]))
            return st

        def prob_steps(t):
            xt, xc = xts[t % 2]
            st = [lambda: norm_stats(k, 8, sqc, rs, G["ones"], 1.0 / D, EPS)]
            for c in range(8):
                st.append(lambda c=c: k.stt("dve", hcs[t % 2][c].a, xc[c].a, gmx.a[:, L * 8 + c:L * 8 + c + 1], rs.a, ALU.mult, ALU.mult,
                                            reads=[xc[c], rs, gmx], writes=[hcs[t % 2][c]]))
            return st

        def rope_tables(t):
            for f in rope_steps(t):
                f()

        pending = []

        def emit(n):
            for _ in range(n):
                if pending:
                    pending.pop(0)()

        pro_load(0)
        pro_a(0)
        pro_b(0)
        rope_tables(0)
        for t in range(NT):
            sl = slice(t * T, (t + 1) * T)
            hc = hcs[t % 2]
            if t + 1 < NT:
                pro_load(t + 1)

            def proj(oc_idx):
                pb = k.ps()
                for kc in range(8):
                    k.mm(pb.a, winc[oc_idx].a[:, kc * 128:(kc + 1) * 128], hc[kc].a, kc == 0, kc == 7,
                         reads=[winc[oc_idx], hc[kc]], writes=[pb])
                return pb

            yat, yac = yats[0]

            def conv(c):
                pb_b = proj(c)
                pb_c = proj(4 + c)
                pb_v = proj(8 + c)
                a_s = acs[0]
                ac = acc[0]
                base = (j * 4 + c) * 3
                k.cp("act", a_s.a, pb_c.a, reads=[pb_c], writes=[a_s])
                k.tt("dve", uc[c].a[:, 2:T + 2], pb_v.a, a_s.a, ALU.mult, reads=[pb_v, a_s], writes=[uc[c]])
                k.ts("dve", ac.a, uc[c].a[:, 0:T], cwb.a[:, base:base + 1], None, ALU.mult, None,
                     reads=[uc[c], cwb], writes=[ac])
                k.stt("dve", ac.a, uc[c].a[:, 1:T + 1], cwb.a[:, base + 1:base + 2], ac.a, ALU.mult, ALU.add,
                      reads=[uc[c], cwb, ac], writes=[ac])
                k.stt("dve", ac.a, uc[c].a[:, 2:T + 2], cwb.a[:, base + 2:base + 3], ac.a, ALU.mult, ALU.add,
                      reads=[uc[c], cwb, ac], writes=[ac])
                k.tt("dve", yac[c].a, pb_b.a, ac.a, ALU.mult, reads=[pb_b, ac], writes=[yac[c]])
                k.cp("act", uc[c].a[:, 0:2], uc[c].a[:, T:T + 2], reads=[uc[c]], writes=[uc[c]])

            pzq = [proj(12 + c) for c in range(3)]
            norm_a(k, 3, [b.a for b in pzq], pzq, sqz)
            conv(0)
            norm_b(k, 3, [b.a for b in pzq], pzq, lambda c: G["gq"].a[:, j * 3 + c:j * 3 + c + 1], 384,
                   sqz, zqn, rsz, G["ones"], G["gq"], in_scale=96.0 / 384.0, bias=EPS * 96.0)
            conv(1)
            pzkv = [proj(15 + c) for c in range(2)]
            norm_a(k, 2, [b.a for b in pzkv], pzkv, sqz)
            conv(2)
            norm_b(k, 2, [b.a for b in pzkv], pzkv, lambda c: G["gkv"].a[:, j * 2 + c:j * 2 + c + 1], 256,
                   sqz, zkvn, rsz, G["ones"], G["gkv"])
            conv(3)
            k.dma("sp", [(cview(YT)[:, 0:4, sl], yat)], reads=yac, sbuf=yac[0])
            if t + 1 < NT:
                pro_a(t + 1)
            ppe = proj(17)
            ppes = proj(18)
            t1 = t1s[0]
            t2 = t2s[0]
            k.tt("dve", t1.a[R, :], ppe.a[R, :], cosT.a[R, :], ALU.mult, reads=[ppe, cosT], writes=[t1])
            k.tt("dve", t2.a[R, :], ppes.a[R, :], sinS.a[R, :], ALU.mult, reads=[ppes, sinS], writes=[t2])
            k.tt("dve", kpe.a[R, :], t1.a[R, :], t2.a[R, :], ALU.add, reads=[t1, t2], writes=[kpe])
            for h in range(8):
                pq = k.ps()
                for kc in range(3):
                    k.mm(pq.a[0:96, :], wq.a[:, h, kc * 96:(kc + 1) * 96], zqn[kc].a, kc == 0, kc == 2,
                         reads=[wq, zqn[kc]], writes=[pq])
                pqs = k.ps()
                for kc in range(3):
                    k.mm(pqs.a[0:96, :], wqs.a[:, h, kc * 96:(kc + 1) * 96], zqn[kc].a, kc == 0, kc == 2,
                         reads=[wqs, zqn[kc]], writes=[pqs])
                k.cp("act", qt.a[0:64, h, :], pq.a[0:64, :], reads=[pq], writes=[qt])
                t1 = t1s[h % 2]
                t2 = t2s[h % 2]
                k.tt("dve", t1.a[R, :], pq.a[R, :], cosT.a[R, :], ALU.mult, reads=[pq, cosT], writes=[t1])
                k.tt("dve", t2.a[R, :], pqs.a[R, :], sinS.a[R, :], ALU.mult, reads=[pqs, sinS], writes=[t2])
                k.tt("pool", qt.a[R, h, :], t1.a[R, :], t2.a[R, :], ALU.add, reads=[t1, t2], writes=[qt])
                if h == 0 and t + 1 < NT:
                    pending.extend(prob_steps(t + 1))
                emit(2)
            k.dma("sp", [(QT.rearrange("h p s -> p h s")[:, :, sl], qt.a)], reads=[qt], sbuf=qt)
            if t + 1 < NT:
                pending.extend(rope_steps(t + 1))
            for h in range(8):
                pk = k.ps()
                for kc in range(2):
                    k.mm(pk.a[0:64, :], wkk.a[:, h, kc * 64:(kc + 1) * 64], zkvn[kc].a, kc == 0, kc == 1,
                         reads=[wkk, zkvn[kc]], writes=[pk])
                k.cp("act", kts.a[0:64, h, :], pk.a[0:64, :], reads=[pk], writes=[kts])
                k.cp("pool", kts.a[R, h, :], kpe.a[R, :], reads=[kpe], writes=[kts])
                emit(2)
            k.dma("sp", [(KT.rearrange("h p s -> p h s")[:, :, sl], kts.a)], reads=[kts], sbuf=kts)
            for b in range(4):
                pv = k.ps()
                for kc in range(2):
                    k.mm(pv.a, zkvn[kc].a[:, b * 128:(b + 1) * 128], wv.a[:, kc * 512:(kc + 1) * 512], kc == 0, kc == 1,
                         reads=[wv, zkvn[kc]], writes=[pv])
                k.cp("act", Vfull[:, t * 4 + b, :, :], pv.a.rearrange("p (h d) -> p h d", d=64),
                     reads=[pv], writes=[Vc[t]])
                emit(3)
            emit(len(pending))


def even_stage2(k, G, YT, QT, KT, Vfull, Vc):
    LA = 2
    with k.stage():
        accs = []
        for _ in range(2):
            t_ = k.es.enter_context(k.nc.psum_tensor(k._name("pa"), [128, 512], F32))
            accs.append(Buf(t_[:]))
        NB = 3
        pss = []
        for _ in range(NB):
            t_ = k.es.enter_context(k.nc.psum_tensor(k._name("pp"), [128, 1024], F32))
            pss.append(Buf(t_[:]))
        qhs = [k.sb([96, S], BF16) for _ in range(2)]
        khs = [k.sb([96, S], BF16) for _ in range(2)]
        vas = [k.sb([128, 32, 128], BF16) for _ in range(2)]
        for va in vas:
            k.op("pool", lambda e: e.memset(va.a[:, :, 64:128], 1.0), writes=[va])
        pts = [k.sb([128, 2 * T], BF16) for _ in range(NB)]
        rden = [k.sb([64, T], F32) for _ in range(2)]
        ybs = [k.sb([64, T], BF16) for _ in range(2)]
        atri = G["atri"]
        iters = [(h, g, kp) for h in range(8) for g in range(NT) for kp in range(2 * g + 2)]
        N = len(iters)

        def front(i):
            h, g, kp = iters[i]
            qh = qhs[h % 2]
            kh = khs[h % 2]
            va = vas[h % 2]
            if g == 0 and kp == 0:
                k.dma("sp", [(qh.a, QT[h])], writes=[qh], sbuf=qh)
                k.dma("sp", [(kh.a, KT[h])], writes=[kh], sbuf=kh)
                k.cp("pool", va.a[:, :, 0:64], Vfull[:, :, h, :], reads=Vc, writes=[va])
            ps2 = pss[i % NB]
            pt = pts[i % NB]
            diag = (2 * kp >= 4 * g)
            los = []
            for u in range(2):
                kb = 2 * kp + u
                lo = max(0, kb - 4 * g) * 128
                los.append(lo)
                k.mm(ps2.a[:, u * T + lo:(u + 1) * T], kh.a[:, kb * 128:(kb + 1) * 128], qh.a[:, g * T + lo:(g + 1) * T],
                     True, True, reads=[kh, qh], writes=[ps2], sig=(u == 1))
            if not diag:
                k.act(pt.a, ps2.a, AF.Exp, reads=[ps2], writes=[pt])
            else:
                for u in range(2):
                    lo = los[u]
                    k.act(pt.a[:, u * T + lo:(u + 1) * T], ps2.a[:, u * T + lo:(u + 1) * T], AF.Exp, reads=[ps2], writes=[pt])
                    k.tt("pool", pt.a[:, u * T + lo:u * T + lo + 128], pt.a[:, u * T + lo:u * T + lo + 128], atri.a, ALU.mult,
                         reads=[pt, atri], writes=[pt])

        def back(i):
            h, g, kp = iters[i]
            va = vas[h % 2]
            nkb = 4 * g + 4
            pt = pts[i % NB]
            gi = h * NT + g
            acc = accs[gi % 2]
            for u in range(2):
                kb = 2 * kp + u
                lo = max(0, kb - 4 * g) * 128
                k.mm(acc.a[:, lo:T], va.a[:, kb, :], pt.a[:, u * T + lo:(u + 1) * T], kb == 0, kb == nkb - 1,
                     reads=[va, pt], writes=[acc], sig=(u == 1))
            if kp == 2 * g + 1:
                sl = slice(g * T, (g + 1) * T)
                rd = rden[gi % 2]
                yb = ybs[gi % 2]
                k.op("dve", lambda e: e.reciprocal(out=rd.a, in_=acc.a[64:128, :]), reads=[acc], writes=[rd])
                k.tt("dve", yb.a, acc.a[0:64, :], rd.a, ALU.mult, reads=[acc, rd], writes=[yb])
                k.dma("sp", [(YT[512 + h * 64:512 + (h + 1) * 64, sl], yb.a)], reads=[yb], sbuf=yb)

        for i in range(N + LA):
            if i < N:
                front(i)
            if i >= LA:
                back(i - LA)


def odd_stage1(k, G, L, j, x_src, YT):
    with k.stage():
        k.ps_alloc(8)
        xts = [k.sbc(8, [T], F32) for _ in range(1)]
        _, sqc = k.sbc(8, [T], BF16)
        hcs = [k.sbc(8, [T], BF16)[1] for _ in range(2)]
        rs = k.sb([128, T], F32)
        rso = k.sb([128, T], F32)
        _, wBc = k.sbc(16, [1024], BF16)
        wgl = k.sb([128, 128], BF16)
        wvA = k.sb([128, 8, 1024], BF16)
        wkA = k.sb([128, 8, 512], BF16)
        wg2 = k.sb([16, 512], BF16)
        bg2 = k.sb([128, 512], F32)
        k.dma("pool", [(wgl.a, G["od_wgl"][j])], writes=[wgl], sbuf=wgl)
        k.dma("pool", [(wg2.a, G["od_wg2"][j])], writes=[wg2], sbuf=wg2)
        k.dma("sp", [(bg2.a, G["od_bg2"][j].partition_broadcast(128))], writes=[bg2], sbuf=bg2)
        k.dma("pool", [(wvA.a, G["od_wvA"][j].rearrange("p (c n) -> p c n", n=1024))], writes=[wvA], sbuf=wvA)
        for oc in list(range(8, 16)) + list(range(0, 8)):
            k.dma("pool", [(wBc[oc].a, G["od_winB"][j, oc])], writes=[wBc[oc]], sbuf=wBc[oc])
        k.dma("pool", [(wkA.a, G["od_wkA"][j].rearrange("p (c n) -> p c n", n=512))], writes=[wkA], sbuf=wkA)
        glT = k.sb([16, T], BF16)
        spc = k.sbc(4, [512], F32)[1]
        Epf, Epc = k.sbc(4, [4, 128], F32)
        Enf, Enc = k.sbc(4, [4, 128], F32)
        El = k.sbc(4, [512], F32)[1]
        qe = k.sbc(4, [T], BF16)[1]
        ke = k.sbc(4, [T], BF16)[1]
        klc = k.sbc(4, [512], BF16)[1]
        vbc = k.sbc(4, [1024], BF16)[1]
        ats = [k.sb([128, 128], BF16) for _ in range(3)]
        Sst = [k.sb([128, 256], F32) for _ in range(4)]
        Sbf = [[k.sb([128, 256], BF16) for _ in range(2)] for _ in range(4)]
        sbi = [0, 0, 0, 0]
        for h in range(4):
            k.op("pool", lambda e: e.memset(Sst[h].a, 0.0), writes=[Sst[h]])
            k.op("pool", lambda e: e.memset(Sbf[h][0].a, 0.0), writes=[Sbf[h][0]])
        oT = k.sbc(8, [T], F32)[1]
        sgs = k.sbc(8, [T], BF16)[1]
        yts = [k.sbc(8, [T], BF16) for _ in range(1)]
        ctri, csup, cmbd = G["ctri"], G["csup"], G["cmbd"]
        gmx = G["gmix"]
        h0 = k.sb([128, 8, 2], F32)
        w32 = [k.sb([128, 1024], F32) for _ in range(1)]
        s00 = k.sb([2, 4], F32)

        def pro_load(t):
            sl = slice(t * T, (t + 1) * T)
            xt, xc = xts[0]
            k.dma("sp", [(xt, cview(x_src)[:, :, sl])], writes=xc, sbuf=xc[0])

        def pro_a(t):
            xt, xc = xts[0]
            norm_a(k, 8, [b.a for b in xc], xc, sqc)

        def pro_b(t):
            xt, xc = xts[0]
            norm_b(k, 8, [b.a for b in xc], xc, lambda c: gmx.a[:, L * 8 + c:L * 8 + c + 1], D,
                   sqc, hcs[t % 2], rs, G["ones"], gmx)

        def pro_b_stats(t):
            norm_stats(k, 8, sqc, rs, G["ones"], 1.0 / D, EPS)

        def pro_b_apply(t, c):
            xt, xc = xts[0]
            k.stt("dve", hcs[t % 2][c].a, xc[c].a, gmx.a[:, L * 8 + c:L * 8 + c + 1], rs.a, ALU.mult, ALU.mult,
                  reads=[xc[c], rs, gmx], writes=[hcs[t % 2][c]])

        pro_load(0)
        pro_a(0)
        pro_b(0)
        xt0, xc0 = xts[0]
        for tk in range(2):
            k.stt("dve", h0.a[:, :, tk], xt0[:, :, tk], rs.a[:, tk:tk + 1], gmx.a[:, L * 8:(L + 1) * 8], ALU.mult, ALU.mult,
                  reads=xc0 + [rs, gmx], writes=[h0])
        pq0 = k.ps()
        pk0 = k.ps()
        for oc in range(8):
            wb = w32[0]
            k.dma("sp", [(wb.a, G["od_winB"][j, oc])], writes=[wb], sbuf=wb)
            dstp = pq0 if oc < 4 else pk0
            for kc in range(8):
                k.mm(dstp.a[0:2, (oc % 4) * 128:(oc % 4 + 1) * 128], h0.a[:, kc, :], wb.a[:, kc * 128:(kc + 1) * 128],
                     kc == 0, kc == 7, reads=[h0, wb], writes=[dstp])
        k.cp("act", El[0].a[0:2, :], pk0.a[0:2, :], reads=[pk0], writes=[El[0]])
        k.tt("dve", rso.a[0:2, :], pq0.a[0:2, :], El[0].a[0:2, :], ALU.mult, reads=[pq0, El[0]], writes=[rso])
        k.op("dve", lambda e: e.reduce_sum(out=s00.a, in_=rso.a[0:2, :].rearrange("p (h d) -> p h d", d=128), axis=mybir.AxisListType.X),
             reads=[rso], writes=[s00])
        k.ts("dve", s00.a, s00.a, 128.0 ** -0.5, None, ALU.mult, None, reads=[s00], writes=[s00])
        for t in range(NT):
            sl = slice(t * T, (t + 1) * T)
            hc = hcs[t % 2]
            if t + 1 < NT:
                pro_load(t + 1)

            def projB(oc_idx):
                pb = k.ps()
                for kc in range(8):
                    k.mm(pb.a, wBc[oc_idx].a[:, kc * 128:(kc + 1) * 128], hc[kc].a, kc == 0, kc == 7,
                         reads=[wBc[oc_idx], hc[kc]], writes=[pb])
                return pb

            pb = k.ps()
            for kc in range(8):
                k.mm(pb.a[0:16, :], wgl.a[:, kc * 16:(kc + 1) * 16], hc[kc].a, kc == 0, kc == 7,
                     reads=[wgl, hc[kc]], writes=[pb])
            k.cp("act", glT.a, pb.a[0:16, :], reads=[pb], writes=[glT])
            for b in range(4):
                bs = slice(b * 128, (b + 1) * 128)
                pb = k.ps()
                k.mm(pb.a, glT.a[:, bs], wg2.a, True, True, reads=[glT, wg2], writes=[pb])
                pr = spc[b]
                k.tt("dve", pr.a, pb.a, bg2.a, ALU.add, reads=[pb, bg2], writes=[pr])
                k.act(pr.a, pr.a, AF.Exp, reads=[pr], writes=[pr], scale=-1.0)
                k.act(pr.a, pr.a, AF.Ln, reads=[pr], writes=[pr], bias=1.0)
            for b in range(4):
                bs = slice(b * 128, (b + 1) * 128)
                for half in range(2):
                    pb = k.ps()
                    for kc in range(8):
                        k.mm(pb.a, hc[kc].a[:, bs], wvA.a[:, kc, half * 512:(half + 1) * 512], kc == 0, kc == 7,
                             reads=[wvA, hc[kc]], writes=[pb])
                    k.cp("act", vbc[b].a[:, half * 512:(half + 1) * 512], pb.a, reads=[pb], writes=[vbc[b]])
            for b in range(4):
                pb = k.ps()
                for h in range(4):
                    k.mm(pb.a[:, h * 128:(h + 1) * 128], spc[b].a[:, h * 128:(h + 1) * 128], ctri.a, True, True,
                         reads=[spc[b], ctri], writes=[pb], sig=(h == 3))
                pv4 = pb.a.rearrange("p (h s) -> p h s", s=128)
                k.act(Epc[b].a, pv4, AF.Exp, reads=[pb], writes=[Epc[b]])
                k.act(Enc[b].a, pv4, AF.Exp, reads=[pb], writes=[Enc[b]], scale=-1.0)
                pb = k.ps()
                k.mm(pb.a, csup.a, spc[b].a, True, True, reads=[csup, spc[b]], writes=[pb])
                k.act(El[b].a, pb.a, AF.Exp, reads=[pb], writes=[El[b]])
            for ci in range(8):
                pg = projB(8 + ci)
                k.act(sgs[ci].a, pg.a, AF.Silu, reads=[pg], writes=[sgs[ci]])
            if t + 1 < NT:
                pro_a(t + 1)
            for h in range(4):
                pq = projB(h)
                k.stt("dve", qe[h].a.rearrange("p (b s) -> p b s", s=128), pq.a.rearrange("p (b s) -> p b s", s=128),
                      128.0 ** -0.5, Epf[:, :, h, :], ALU.mult, ALU.mult, reads=[pq] + Epc, writes=[qe[h]])
                pk = projB(4 + h)
                k.tt("dve", ke[h].a.rearrange("p (b s) -> p b s", s=128), pk.a.rearrange("p (b s) -> p b s", s=128),
                     Enf[:, :, h, :], ALU.mult, reads=[pk] + Enc, writes=[ke[h]])
            for b in range(4):
                bs = slice(b * 128, (b + 1) * 128)
                pb = k.ps()
                for kc in range(8):
                    k.mm(pb.a, hc[kc].a[:, bs], wkA.a[:, kc, :], kc == 0, kc == 7, reads=[wkA, hc[kc]], writes=[pb])
                k.tt("dve", klc[b].a, pb.a, El[b].a, ALU.mult, reads=[pb, El[b]], writes=[klc[b]])
            if t + 1 < NT:
                pro_b_stats(t + 1)
            ai = 0
            for b in range(4):
                bs = slice(b * 128, (b + 1) * 128)
                for h in range(4):
                    pds = []
                    for c in range(2):
                        rows = slice(c * 64, (c + 1) * 64)
                        p_ = k.ps()
                        k.mm(p_.a[:, 0:256], klc[b].a[rows, h * 128:(h + 1) * 128], vbc[b].a[rows, h * 256:(h + 1) * 256],
                             True, True, reads=[klc[b], vbc[b]], writes=[p_])
                        pds.append(p_)
                    sb0 = Sbf[h][sbi[h] % 2]
                    sb1 = Sbf[h][(sbi[h] + 1) % 2]
                    dec0 = Epf[:, b, h, 63:64]
                    dec1 = Epf[:, b, h, 127:128]
                    k.stt("dve", Sst[h].a, Sst[h].a, dec0, pds[0].a[:, 0:256], ALU.mult, ALU.add,
                          reads=[Sst[h], Epc[b], pds[0]], writes=[Sst[h]])
                    k.cp("act", sb1.a, Sst[h].a, reads=[Sst[h]], writes=[sb1])
                    pat = k.ps()
                    k.mm(pat.a[:, 0:128], ke[h].a[:, bs], qe[h].a[:, bs], True, True, reads=[ke[h], qe[h]], writes=[pat])
                    at = ats[ai % 3]
                    ai += 1
                    k.tt("dve", at.a, pat.a[:, 0:128], cmbd.a, ALU.mult, reads=[pat, cmbd], writes=[at])
                    if t == 0 and b == 0:
                        k.cp("dve", at.a[0:1, 0:1], s00.a[0:1, h:h + 1], reads=[s00, at], writes=[at])
                    po = [k.ps(), k.ps()]
                    for dvc in range(2):
                        k.mm(po[dvc].a[:, 0:64], sb0.a[:, dvc * 128:(dvc + 1) * 128], qe[h].a[:, b * 128:b * 128 + 64],
                             True, False, reads=[sb0, qe[h]], writes=[po[dvc]], sig=False)
                    for dvc in range(2):
                        k.mm(po[dvc].a[:, 64:128], sb1.a[:, dvc * 128:(dvc + 1) * 128], qe[h].a[:, b * 128 + 64:b * 128 + 128],
                             False, False, reads=[sb1, qe[h]], writes=[po[dvc]], sig=False)
                    for dvc in range(2):
                        k.mm(po[dvc].a[:, 0:128], vbc[b].a[:, h * 256 + dvc * 128:h * 256 + (dvc + 1) * 128], at.a,
                             False, True, reads=[vbc[b], at], writes=[po[dvc]], sig=True)
                    k.stt("dve", Sst[h].a, Sst[h].a, dec1, pds[1].a[:, 0:256], ALU.mult, ALU.add,
                          reads=[Sst[h], Epc[b], pds[1]], writes=[Sst[h]])
                    k.cp("act", sb0.a, Sst[h].a, reads=[Sst[h]], writes=[sb0])
                    for dvc in range(2):
                        k.cp("act", oT[h * 2 + dvc].a[:, bs], po[dvc].a[:, 0:128], reads=[po[dvc]], writes=[oT[h * 2 + dvc]])
                    if t + 1 < NT and h % 2 == 1:
                        pro_b_apply(t + 1, b * 2 + h // 2)
            yt, yc = yts[0]
            for h in range(4):
                srcs = [oT[h * 2], oT[h * 2 + 1]]
                norm_a(k, 2, [b_.a for b_ in srcs], srcs, sqc)
                norm_b(k, 2, [b_.a for b_ in srcs], srcs, lambda c: G["go"].a[:, j * 2 + c:j * 2 + c + 1], 256,
                       sqc, srcs, rso, G["ones"], G["go"])
                for dvc in range(2):
                    ci = h * 2 + dvc
                    k.tt("dve", yc[ci].a, oT[ci].a, sgs[ci].a, ALU.mult, reads=[oT[ci], sgs[ci]], writes=[yc[ci]])
            k.dma("sp", [(cview(YT)[:, :, sl], yt)], reads=yc, sbuf=yc[0])


def build_program(nlayers=4):
    nc = bass.Bass("TRN2", target_bir_lowering=False)

    def din(name, shape, dtype=F32):
        return nc.dram_tensor(name, list(shape), dtype, kind="ExternalInput").ap()

    xT = din("xT", [D, S])
    G = {}
    G["pos"] = din("pos", [1, S], I32)
    smalls = {"gmix": 32, "gmlp": 32, "gfin": 8, "gq": 6, "gkv": 4, "go": 4, "cw": 24, "crope": 2}
    small_in = {n: din(n, [128, w]) for n, w in smalls.items()}
    consts_in = {n: din(n, [128, 128]) for n in ("ctri", "csup", "cmbd", "atri")}
    G["ev_win"] = din("ev_win", [2, 19, 128, 1024])
    G["ev_wq"] = din("ev_wq", [2, 8, 128, 288])
    G["ev_wqs"] = din("ev_wqs", [2, 8, 128, 288])
    G["ev_wkk"] = din("ev_wkk", [2, 8, 128, 128])
    G["ev_wv"] = din("ev_wv", [2, 128, 1024])
    G["ev_wo"] = din("ev_wo", [2, 8, 128, 1024])
    G["od_winB"] = din("od_winB", [2, 16, 128, 1024])
    G["od_wgl"] = din("od_wgl", [2, 128, 128])
    G["od_wvA"] = din("od_wvA", [2, 128, 8192])
    G["od_wkA"] = din("od_wkA", [2, 128, 4096])
    G["od_wg2"] = din("od_wg2", [2, 16, 512])
    G["od_bg2"] = din("od_bg2", [2, 1, 512])
    G["od_wo"] = din("od_wo", [2, 8, 128, 1024])
    G["w1"] = din("w1", [4, 32, 128, 1024])
    G["w2"] = din("w2", [4, 8, 128, 4096])
    outT = nc.dram_tensor("outT", [D, S], F32, kind="ExternalOutput").ap()
    xs = nc.dram_tensor("xs", [D, S], F32).ap()
    YT = nc.dram_tensor("YTs", [D, S], BF16).ap()
    QT = nc.dram_tensor("QTs", [8, 96, S], BF16).ap()
    KT = nc.dram_tensor("KTs", [8, 96, S], BF16).ap()

    k = KB(nc)
    with k.es:
        k.es_root = k.es
        for n, w in smalls.items():
            b = k.sb([128, w], F32)
            k.dma("sp", [(b.a, small_in[n])], writes=[b], sbuf=b)
            G[n] = b
        for n in ("ctri", "csup", "cmbd"):
            b = k.sb([128, 128], F32)
            k.dma("sp", [(b.a, consts_in[n])], writes=[b], sbuf=b)
            G[n] = b
        b = k.sb([128, 128], BF16)
        k.dma("pool", [(b.a, consts_in["atri"])], writes=[b], sbuf=b)
        G["atri"] = b
        ones = k.sb([128, 128], BF16)
        k.op("pool", lambda e: e.memset(ones.a, 1.0), writes=[ones])
        G["ones"] = ones
        hpi = k.sb([128, 1], F32)
        k.op("pool", lambda e: e.memset(hpi.a, math.pi / 2), writes=[hpi])
        G["hpi"] = hpi
        G["negpi"] = hpi
        k.barrier()
        x_cur = xT
        for L in range(nlayers):
            j = L // 2
            last = (L == nlayers - 1)
            if L % 2 == 0:
                with k.stage():
                    vt = k.es.enter_context(nc.sbuf_tensor(k._name("Vres"), [128, 32, 8, 64], BF16))
                    Vfull = vt[:]
                    Vc = [Buf(Vfull[:, t * 4:(t + 1) * 4]) for t in range(NT)]
                    sbufs = k.stage_bufs
                    even_stage1(k, G, L, j, x_cur, YT, QT, KT, Vfull, Vc)
                    k.stage_bufs = sbufs
                    even_stage2(k, G, YT, QT, KT, Vfull, Vc)
                    k.stage_bufs = sbufs
                mlp_stage(k, G, L, x_cur, xs, G["ev_wo"][j], YT, last, outT)
            else:
                odd_stage1(k, G, L, j, x_cur, YT)
                mlp_stage(k, G, L, x_cur, xs, G["od_wo"][j], YT, last, outT)
            x_cur = xs
        k.barrier()
    return nc


def _bform(w, m=128):
    K, N = w.shape
    kc, oc = K // 128, N // m
    return np.ascontiguousarray(w.reshape(kc, 128, oc, m).transpose(2, 1, 0, 3).reshape(oc, 128, kc * m))


def _aform(w):
    K, N = w.shape
    kc = K // 128
    return np.ascontiguousarray(w.reshape(kc, 128, N).transpose(1, 0, 2).reshape(128, kc * N))


def _pvec(g):
    g2 = g.reshape(-1, g.shape[-1] // 128, 128)
    return np.ascontiguousarray(g2.transpose(2, 0, 1).reshape(128, -1))


def prepare_shared(inp):
    f = lambda a: np.asarray(a, dtype=np.float32)
    sh = {}
    sh["gmix"] = _pvec(f(inp["mix_norm_g"]))
    sh["gmlp"] = _pvec(f(inp["mlp_norm_g"]))
    sh["gfin"] = _pvec(f(inp["final_norm_g"])[None])
    sh["gq"] = _pvec(f(inp["ev_q_norm_g"]))
    sh["gkv"] = _pvec(f(inp["ev_kv_norm_g"]))
    sh["go"] = _pvec(f(inp["od_o_norm_g"]))
    cw = f(inp["ev_conv_w"])
    sh["cw"] = np.ascontiguousarray(cw.reshape(2, 3, 4, 128).transpose(3, 0, 2, 1).reshape(128, 24))
    inv_freq = (1.0 / (10000.0 ** (np.arange(0, 32, 2, dtype=np.float32) / np.float32(32)))).astype(np.float32)
    crope = np.zeros((128, 2), np.float32)
    crope[64:80, 0] = inv_freq
    crope[80:96, 0] = inv_freq
    crope[64:80, 1] = -1.0
    crope[80:96, 1] = 1.0
    sh["crope"] = crope
    jj = np.arange(128)[:, None]
    ii = np.arange(128)[None, :]
    same = (jj // 64) == (ii // 64)
    sh["ctri"] = np.where(same & (jj <= ii), -1.0 / 16.0, 0.0).astype(np.float32)
    sh["csup"] = np.where(same & (jj > ii), -1.0 / 16.0, 0.0).astype(np.float32)
    sh["cmbd"] = np.where(same & (jj <= ii), 1.0, 0.0).astype(np.float32)
    sh["atri"] = np.where(jj <= ii, 1.0, 0.0).astype(np.float32)
    ev_win, ev_wq, ev_wqs, ev_wkk, ev_wv, ev_wo = [], [], [], [], [], []
    for j in range(2):
        w = f(inp["ev_w_in"][j])
        pe = w[:, 2176:2208]
        pesw = np.concatenate([pe[:, 16:32], pe[:, 0:16]], axis=1)
        wext = np.concatenate([w[:, :2176], pe, pe, pe, pe[:, 0:32], pe, pe, pesw, pe[:, 0:32]], axis=1)
        assert wext.shape[1] == 19 * 128
        ev_win.append(_bform(wext))
        wq = f(inp["ev_w_qb"][j]).reshape(384, 8, 96)
        wqs = np.concatenate([wq[:, :, 0:64], wq[:, :, 80:96], wq[:, :, 64:80]], axis=2)
        ev_wq.append(_bform(wq.reshape(384, 768), 96))
        ev_wqs.append(_bform(wqs.reshape(384, 768), 96))
        wkv = f(inp["ev_w_kvb"][j]).reshape(256, 8, 128)
        ev_wkk.append(_bform(np.ascontiguousarray(wkv[:, :, 0:64]).reshape(256, 512), 64))
        ev_wv.append(_aform(np.ascontiguousarray(wkv[:, :, 64:128]).reshape(256, 512)))
        ev_wo.append(_bform(f(inp["ev_w_out"][j])))
    sh["ev_win"] = np.stack(ev_win)
    sh["ev_wq"] = np.stack(ev_wq)
    sh["ev_wqs"] = np.stack(ev_wqs)
    sh["ev_wkk"] = np.stack(ev_wkk)
    sh["ev_wv"] = np.stack(ev_wv)
    sh["ev_wo"] = np.stack(ev_wo)
    od_winB, od_wgl, od_wvA, od_wkA, od_wo = [], [], [], [], []
    for j in range(2):
        w = f(inp["od_w_in"][j])
        od_winB.append(_bform(np.concatenate([w[:, 0:1024], w[:, 2048:3072]], axis=1)))
        od_wgl.append(_bform(w[:, 3072:3088], 16)[0])
        od_wvA.append(_aform(w[:, 1024:2048]))
        od_wkA.append(_aform(w[:, 512:1024]))
        od_wo.append(_bform(f(inp["od_w_out"][j])))
    sh["od_winB"] = np.stack(od_winB)
    sh["od_wgl"] = np.stack(od_wgl)
    sh["od_wvA"] = np.stack(od_wvA)
    sh["od_wkA"] = np.stack(od_wkA)
    sh["od_wg2"] = np.ascontiguousarray(f(inp["od_w_gate2"]))
    sh["od_bg2"] = np.ascontiguousarray(f(inp["od_b_gate2"]).reshape(2, 1, 512))
    sh["od_wo"] = np.stack(od_wo)
    sh["w1"] = np.stack([_bform(f(inp["mlp_w1"][l])) for l in range(4)])
    sh["w2"] = np.stack([_bform(f(inp["mlp_w2"][l])) for l in range(4)])
    return sh


def run(inp, nlayers=4, ncores=8):
    sh = prepare_shared(inp)
    x = np.asarray(inp["x"], dtype=np.float32)
    pos = np.asarray(inp["positions"], dtype=np.int32)
    nc = build_program(nlayers)
    in_maps = []
    for b in range(ncores):
        m = dict(sh)
        m["xT"] = np.ascontiguousarray(x[b].T)
        m["pos"] = np.ascontiguousarray(pos[b][None, :])
        in_maps.append(m)
    res = run_bass_kernel_spmd(nc, in_maps, core_ids=list(range(ncores)))
    return np.stack([np.ascontiguousarray(r["outT"].T) for r in res.results], axis=0)


def kernel(**inputs):
    return run(inputs, 4, 8).astype(np.float32)
```
